# Optimizing a Trainium2 kernel written in Bass

```python
import jax, jax.numpy as jnp
from jax import lax
import numpy as np

D_MODEL = 1024
BATCH = 4
SEQ = 8192
DEPTH = 1

RET_HEADS = 4
RET_DK = 128
RET_DV = 256
RET_CHUNK = 128
ROPE_BASE = 10000.0
MOBA_HEADS = 8
MOBA_DH = 64
MOBA_BLOCK = 256
MOBA_TOPK = 3
MOBA_QBLOCK = 64
FFN_HIDDEN = ((8 * D_MODEL + 3 * 256 - 1) // (3 * 256)) * 256

RET_QK = RET_HEADS * RET_DK
RET_V = RET_HEADS * RET_DV
MOBA_W = MOBA_HEADS * MOBA_DH
IN_SPLITS = (RET_QK, RET_QK, RET_V, RET_V, MOBA_W, MOBA_W, MOBA_W, D_MODEL, D_MODEL)
IN_COLS = sum(IN_SPLITS)

RMS_EPS = 1e-6
GN_EPS = 1e-5
NEG = -1e30

kernel_name = "hybrid_retention_moba_block"


def rmsnorm(x, w):
    xf = x.astype(jnp.float32)
    y = xf * lax.rsqrt(jnp.mean(xf * xf, axis=-1, keepdims=True) + RMS_EPS)
    return (y * w.astype(jnp.float32)).astype(x.dtype)


def rotary(x, pos):
    half = x.shape[-1] // 2
    inv = ROPE_BASE ** (-jnp.arange(half, dtype=jnp.float32) / half)
    ang = pos.astype(jnp.float32)[:, None] * inv[None, :]
    cos = jnp.cos(ang)[None, :, None, :].astype(x.dtype)
    sin = jnp.sin(ang)[None, :, None, :].astype(x.dtype)
    x1, x2 = x[..., :half], x[..., half:]
    return jnp.concatenate([x1 * cos - x2 * sin, x2 * cos + x1 * sin], axis=-1)


def retention(q, k, v):
    B, S, H, _ = q.shape
    dv = v.shape[-1]
    C = RET_CHUNK
    NC = S // C
    dt = q.dtype

    def chunks(t):
        return t.reshape(B, NC, C, H, t.shape[-1]).transpose(0, 3, 1, 2, 4)

    qc, kc, vc = chunks(q), chunks(k), chunks(v)
    lg = jnp.log1p(-jnp.exp2(-5.0 - jnp.arange(H, dtype=jnp.float32)))
    idx = jnp.arange(C, dtype=jnp.float32)
    diff = idx[:, None] - idx[None, :]
    dmat = jnp.where(diff >= 0, jnp.exp(jnp.maximum(diff, 0.0)[None] * lg[:, None, None]), 0.0).astype(dt)
    xi = jnp.exp((idx + 1.0)[None, :] * lg[:, None]).astype(dt)
    zeta = jnp.exp((C - 1.0 - idx)[None, :] * lg[:, None]).astype(dt)
    g_chunk = jnp.exp(C * lg).astype(dt)

    scores = jnp.einsum('bhcnd,bhcmd->bhcnm', qc, kc) * dmat[None, :, None]
    o_inner = jnp.einsum('bhcnm,bhcme->bhcne', scores, vc)
    upd = jnp.einsum('bhcmd,bhcme->bhcde', kc * zeta[None, :, None, :, None], vc)
    upd = jnp.moveaxis(upd, 2, 0)

    def step(state, u):
        return g_chunk[None, :, None, None] * state + u, state

    _, r_prev = lax.scan(step, jnp.zeros_like(upd[0]), upd)
    r_prev = jnp.moveaxis(r_prev, 0, 2)
    o_cross = jnp.einsum('bhcnd,bhcde->bhcne', qc * xi[None, :, None, :, None], r_prev)
    o = (o_inner + o_cross).transpose(0, 2, 3, 1, 4).reshape(B, S, H, dv)
    of = o.astype(jnp.float32)
    mu = jnp.mean(of, axis=-1, keepdims=True)
    var = jnp.mean(jnp.square(of - mu), axis=-1, keepdims=True)
    o = ((of - mu) * lax.rsqrt(var + GN_EPS)).astype(dt)
    return o.reshape(B, S, H * dv)


def gather_blocks(blocks, sel):
    return jax.vmap(jax.vmap(lambda bl, ix: bl[ix]))(blocks, sel)


def moba_attention(q, k, v):
    B, S, H, dh = q.shape
    BLK, QB = MOBA_BLOCK, MOBA_QBLOCK
    sp = ((S + BLK - 1) // BLK) * BLK
    pad = ((0, 0), (0, sp - S), (0, 0), (0, 0))
    qh = jnp.pad(q, pad).transpose(0, 2, 1, 3)
    kh = jnp.pad(k, pad).transpose(0, 2, 1, 3)
    vh = jnp.pad(v, pad).transpose(0, 2, 1, 3)
    nb = sp // BLK
    kb = kh.reshape(B, H, nb, BLK, dh)
    vb = vh.reshape(B, H, nb, BLK, dh)
    kbar = jnp.mean(kb, axis=3)
    topk = min(MOBA_TOPK, nb)
    scale = dh ** -0.5
    n_qb = sp // QB

    def one_block(qi):
        start = qi * QB
        blk = start // BLK
        qb = lax.dynamic_slice_in_dim(qh, start, QB, axis=2)
        gate = jnp.einsum('bhqd,bhnd->bhqn', qb, kbar).astype(jnp.float32)
        gate = jnp.where(jnp.arange(nb)[None, None, None, :] < blk, gate, NEG)
        _, sel = lax.top_k(gate, topk)
        valid = sel < blk
        ks = gather_blocks(kb, sel)
        vs = gather_blocks(vb, sel)
        s_sel = jnp.einsum('bhqd,bhqkjd->bhqkj', qb, ks).astype(jnp.float32) * scale
        s_sel = jnp.where(valid[..., None], s_sel, NEG).reshape(B, H, QB, topk * BLK)
        k_own = lax.dynamic_index_in_dim(kb, blk, axis=2, keepdims=False)
        v_own = lax.dynamic_index_in_dim(vb, blk, axis=2, keepdims=False)
        s_own = jnp.einsum('bhqd,bhjd->bhqj', qb, k_own).astype(jnp.float32) * scale
        qpos = start + jnp.arange(QB)
        kpos = blk * BLK + jnp.arange(BLK)
        s_own = jnp.where(kpos[None, :] <= qpos[:, None], s_own, NEG)
        p = jax.nn.softmax(jnp.concatenate([s_sel, s_own], axis=-1), axis=-1)
        p_sel = p[..., :topk * BLK].reshape(B, H, QB, topk, BLK).astype(v.dtype)
        p_own = p[..., topk * BLK:].astype(v.dtype)
        return (jnp.einsum('bhqkj,bhqkjd->bhqd', p_sel, vs)
                + jnp.einsum('bhqj,bhjd->bhqd', p_own, v_own))

    out = lax.map(one_block, jnp.arange(n_qb))
    out = out.transpose(1, 0, 3, 2, 4).reshape(B, sp, H * dh)
    return out[:, :S]


def setup_inputs(seed: int = 0) -> dict:
    key = jax.random.key(seed)
    ks = jax.random.split(key, 12)

    def w(k, shape, fan_in):
        return jax.random.normal(k, shape, jnp.float32) * fan_in ** -0.5

    def gain(k, shape):
        return 1.0 + 0.02 * jax.random.normal(k, shape, jnp.float32)

    return {
        "x": jax.random.normal(ks[0], (BATCH, SEQ, D_MODEL), jnp.float32),
        "norm1_w": gain(ks[1], (DEPTH, D_MODEL)),
        "w_in": w(ks[2], (DEPTH, D_MODEL, IN_COLS), D_MODEL),
        "q_norm_w": gain(ks[3], (DEPTH, MOBA_DH)),
        "k_norm_w": gain(ks[4], (DEPTH, MOBA_DH)),
        "w_ret_out": w(ks[5], (DEPTH, RET_V, D_MODEL), RET_V),
        "w_moba_out": w(ks[6], (DEPTH, MOBA_W, D_MODEL), MOBA_W),
        "w_o": w(ks[7], (DEPTH, D_MODEL, D_MODEL), D_MODEL),
        "norm2_w": gain(ks[8], (DEPTH, D_MODEL)),
        "w_ffn_gate": w(ks[9], (DEPTH, D_MODEL, FFN_HIDDEN), D_MODEL),
        "w_ffn_up": w(ks[10], (DEPTH, D_MODEL, FFN_HIDDEN), D_MODEL),
        "w_ffn_down": w(ks[11], (DEPTH, FFN_HIDDEN, D_MODEL), FFN_HIDDEN),
    }


def reference(x, norm1_w, w_in, q_norm_w, k_norm_w, w_ret_out, w_moba_out, w_o,
              norm2_w, w_ffn_gate, w_ffn_up, w_ffn_down):
    B, S, _ = x.shape
    pos = jnp.arange(S)
    split_points = [int(v) for v in np.cumsum(IN_SPLITS)[:-1]]
    for l in range(DEPTH):
        h = rmsnorm(x, norm1_w[l])
        proj = h @ w_in[l]
        rq, rk, rv, rg, mq, mk, mv, ga, gb = jnp.split(proj, split_points, axis=-1)

        rq = rotary(rq.reshape(B, S, RET_HEADS, RET_DK), pos)
        rk = rotary(rk.reshape(B, S, RET_HEADS, RET_DK), pos) * (RET_DK ** -0.5)
        rv = rv.reshape(B, S, RET_HEADS, RET_DV)
        ret = jax.nn.silu(rg) * retention(rq, rk, rv)
        a = ret @ w_ret_out[l]

        mq = rmsnorm(mq.reshape(B, S, MOBA_HEADS, MOBA_DH), q_norm_w[l])
        mk = rmsnorm(mk.reshape(B, S, MOBA_HEADS, MOBA_DH), k_norm_w[l])
        mv = mv.reshape(B, S, MOBA_HEADS, MOBA_DH)
        b = moba_attention(mq, mk, mv) @ w_moba_out[l]

        mix = jax.nn.sigmoid(ga) * a + jax.nn.sigmoid(gb) * b
        x = x + mix @ w_o[l]

        h2 = rmsnorm(x, norm2_w[l])
        f = (jax.nn.silu(h2 @ w_ffn_gate[l]) * (h2 @ w_ffn_up[l])) @ w_ffn_down[l]
        x = x + f
    return x
```

```python
import numpy as np
import ml_dtypes
from contextlib import ExitStack
import concourse.bass as bass
import concourse.mybir as mybir
from concourse.bass_utils import run_bass_kernel_spmd

F32 = mybir.dt.float32
BF16 = mybir.dt.bfloat16
AF = mybir.ActivationFunctionType
ALU = mybir.AluOpType
AX = mybir.AxisListType

EPOCH = 30000
N_EPOCH = 4
N_DMA_SEM = 48

D = 1024
S_HALF = 4096
SV = 8192
SEG = 512
NSEG = SV // SEG
IN_COLS = 6656
FFN = 2816
NKC_F = FFN // 128
NSLOT = 3
NEGB = 30000.0


class Buf:
    __slots__ = ("name", "w", "r")

    def __init__(self, name):
        self.name = name
        self.w = None
        self.r = []


class TB:
    __slots__ = ("t", "b")

    def __init__(self, t, name):
        self.t = t
        self.b = Buf(name)


class Emitter:
    def __init__(self, nc, stack):
        self.nc = nc
        self.eng = {"pe": nc.tensor, "act": nc.scalar, "dve": nc.vector,
                    "pool": nc.gpsimd, "sp": nc.sync}
        self.cnt = {e: 0 for e in ("pe", "act", "dve", "pool")}
        self.sems = {e: [stack.enter_context(nc.semaphore(f"s_{e}{i}")) for i in range(N_EPOCH)]
                     for e in ("pe", "act", "dve", "pool")}
        self.dsem = [stack.enter_context(nc.semaphore(f"s_dma{i}")) for i in range(N_DMA_SEM)]
        self.dval = [0] * N_DMA_SEM
        self.dnext2 = [0, 0]
        self.waited = {e: {} for e in self.eng}

    def _wait(self, e, ref):
        if ref[0] == "eng":
            _, pe_, n = ref
            key = ("eng", pe_)
            if self.waited[e].get(key, 0) >= n:
                return
            self.waited[e][key] = n
            self.eng[e].wait_ge(self.sems[pe_][(n - 1) // EPOCH], (n - 1) % EPOCH + 1)
        else:
            _, si, val = ref
            key = ("dma", si)
            if self.waited[e].get(key, 0) >= val:
                return
            self.waited[e][key] = val
            self.eng[e].wait_ge(self.dsem[si], val)

    def _deps(self, e, reads, writes, is_dma=False):
        deps = []
        for b in list(reads) + list(writes):
            if b.w is not None:
                deps.append(b.w)
        for b in writes:
            deps.extend(b.r)
        for ref in deps:
            if ref[0] == "eng" and ref[1] == e and e == "pe" and not is_dma:
                continue
            self._wait(e, ref)

    def op(self, e, fn, reads=(), writes=(), inc=True):
        self._deps(e, reads, writes)
        ins = fn()
        n = self.cnt[e] + 1
        ref = ("eng", e, n)
        if inc:
            self.cnt[e] = n
            ins.then_inc(self.sems[e][(n - 1) // EPOCH], 1)
        for b in reads:
            b.r.append(ref)
            if len(b.r) > 64:
                b.r = b.r[-64:] if False else b.r
        for b in writes:
            b.w = ref
            b.r = []
        return ins

    def dma(self, q, out, in_, reads=(), writes=(), **kw):
        half = N_DMA_SEM // 2
        k = 1 if q == "pool" else 0
        si = k * half + self.dnext2[k]
        self.dnext2[k] = (self.dnext2[k] + 1) % half
        if self.dval[si] > 0:
            self._wait(q, ("dma", si, self.dval[si]))
        self._deps(q, reads, writes, is_dma=True)
        self.dval[si] += 16
        ref = ("dma", si, self.dval[si])
        self.eng[q].dma_start(out=out, in_=in_, **kw).then_inc(self.dsem[si], 16)
        for b in reads:
            b.r.append(ref)
        for b in writes:
            b.w = ref
            b.r = []
        return ref

    def barrier(self):
        for e in self.eng:
            for pe_ in self.cnt:
                if pe_ != e and self.cnt[pe_] > 0:
                    self._wait(e, ("eng", pe_, self.cnt[pe_]))
            for si in range(N_DMA_SEM):
                if self.dval[si] > 0:
                    self._wait(e, ("dma", si, self.dval[si]))


C_RQ, C_RK, C_RV, C_RG, C_MQ, C_MK, C_MV, C_GA, C_GB = 0, 512, 1024, 2048, 3072, 3584, 4096, 4608, 5632


def granule_table():
    g = []
    A = lambda src, cols, perm=False: g.append(("A", src, cols, perm))
    B = lambda src, c0, k0=0, nk=8: g.append(("B", src, c0, k0, nk))
    A("w_in", [C_RK + 128 * h for h in range(4)])
    A("w_in", [C_RK + 128 * h for h in range(4)], True)
    A("w_in", [C_MK + 128 * u for u in range(4)])
    B("w_in", C_RV); B("w_in", C_RV + 512)
    B("w_in", C_MV)
    A("w_in", [C_RQ + 128 * h for h in range(4)])
    A("w_in", [C_RQ + 128 * h for h in range(4)], True)
    A("w_in", [C_MQ + 128 * u for u in range(4)])
    A("w_in", [C_GA + 128 * u for u in range(4)]); A("w_in", [C_GA + 512 + 128 * u for u in range(4)])
    A("w_in", [C_GB + 128 * u for u in range(4)]); A("w_in", [C_GB + 512 + 128 * u for u in range(4)])
    B("w_in", C_RG); B("w_in", C_RG + 512)
    A("w_ret_out", [128 * u for u in range(4)]); A("w_ret_out", [512 + 128 * u for u in range(4)])
    g.append(("C", "w_moba_out", [128 * u for u in range(4)]))
    g.append(("C", "w_moba_out", [512 + 128 * u for u in range(4)]))
    B("w_o", 0); B("w_o", 512)
    for i in range(6):
        A("w_ffn_gate", [128 * u for u in range(4 * i, min(4 * i + 4, NKC_F))])
    for i in range(6):
        A("w_ffn_up", [128 * u for u in range(4 * i, min(4 * i + 4, NKC_F))])
    for hf in range(2):
        for k0, nk in ((0, 8), (8, 8), (16, 6)):
            B("w_ffn_down", hf * 512, k0, nk)
    return g


G_CHUNK = [float(np.exp(np.float32(128.0) * np.log1p(-np.exp2(np.float32(-5.0 - h))))) for h in range(4)]
GRAN = granule_table()
NG = len(GRAN)
G_GATE, G_UP, G_DOWN = 21, 27, 33


def build(dbg=False):
    nc = bass.Bass("TRN2", target_bir_lowering=False)
    IN = lambda name, shape, dt=F32: nc.dram_tensor(name, shape, dt, kind="ExternalInput").ap()
    SCR = lambda name, shape, dt=BF16: nc.dram_tensor(
        name, shape, dt, kind=("ExternalOutput" if dbg else "Internal")).ap()
    xv = IN("xv", [SV, D])
    wsrc = {"w_in": IN("w_in", [D, IN_COLS]), "w_ret_out": IN("w_ret_out", [D, D]),
            "w_moba_out": IN("w_moba_out", [512, D]), "w_o": IN("w_o", [D, D]),
            "w_ffn_gate": IN("w_ffn_gate", [D, FFN]), "w_ffn_up": IN("w_ffn_up", [D, FFN]),
            "w_ffn_down": IN("w_ffn_down", [FFN, D])}
    n1w = IN("norm1_w", [1, D]); n2w = IN("norm2_w", [1, D])
    qnw = IN("q_norm_w", [64, 1]); knw = IN("k_norm_w", [64, 1])
    cosk_d = IN("cosk", [128, SV]); sink_d = IN("sink", [128, SV])
    cosq_d = IN("cosq", [128, S_HALF]); sinq_d = IN("sinq", [128, S_HALF])
    gbias_d = IN("gbias", [S_HALF, 32]); ownsel_d = IN("ownsel", [S_HALF, 32])
    dmat_d = IN("dmatT", [128, 512]); xi_d = IN("xitab", [128, 512]); zeta_d = IN("zetatab", [128, 512])
    ident_d = IN("ident", [128, 128], BF16); cm_d = IN("cmask", [128, 512], BF16)
    eoh_d = IN("eonehot", [32, SV], BF16)
    out_d = nc.dram_tensor("out", [S_HALF, D], F32, kind="ExternalOutput").ap()

    Wg = nc.dram_tensor("Wg", [NG, 128, 4096], BF16, kind="Internal").ap()
    KT = SCR("KT", [8, 64, SV])
    QS = SCR("QS", [8, 96, S_HALF])
    OT = SCR("OT", [8, 64, S_HALF])
    MIXA = SCR("MIXA", [128, 8, S_HALF])
    SGB = SCR("SGB", [128, 8, S_HALF])
    B_Wg = [Buf(f"Wg{i}") for i in range(NG)]
    B_KT = [Buf(f"KT{h}") for h in range(8)]
    B_QS = [Buf(f"QS{h}") for h in range(8)]
    B_OT = Buf("OT"); B_MIXA = Buf("MIXA"); B_SGB = Buf("SGB"); B_out = Buf("out")

    with ExitStack() as st0:
        em = Emitter(nc, st0)
        op = em.op

        def SB(st, name, shape, dt):
            return TB(st.enter_context(nc.sbuf_tensor("sb_" + name, shape, dt)), name)

        def PS(st, name, shape, dt):
            return TB(st.enter_context(nc.psum_tensor("ps_" + name, shape, dt)), name)

        psF2 = [PS(st0, f"psF2_{i}", [128, 1024], F32) for i in range(3)]
        psF = [TB(psF2[i // 2].t[:, (i % 2) * 512:(i % 2 + 1) * 512], f"psF{i}") for i in range(6)]
        psB = [PS(st0, f"psB{i}", [128, 1024], BF16) for i in range(2)]
        pfi = [0]; pbi = [0]

        def pf(n=4):
            i = pfi[0] % n
            pfi[0] += 1
            return psF[i]

        def pb():
            i = pbi[0] % 2
            pbi[0] += 1
            return psB[i]

        wslot = [SB(st0, f"wslot{i}", [128, 4096], BF16) for i in range(NSLOT)]
        ident = SB(st0, "ident", [128, 128], BF16)
        onesf = SB(st0, "onesf", [128, 64], F32)
        onesb = SB(st0, "onesb", [64, 64], BF16)
        wn1 = SB(st0, "wn1", [128, D], F32)
        gq = SB(st0, "gq", [64, 1], F32)
        gk = SB(st0, "gk", [64, 1], F32)
        gk2 = SB(st0, "gk2", [128, 1], F32)
        onesblk = SB(st0, "onesblk", [128, 128], BF16)

        def cast_granule(gi):
            gd = GRAN[gi]
            W = wsrc[gd[1]]
            if gd[0] == "A":
                dst = Wg[gi].rearrange("p (kc c) -> p kc c", kc=8)
                if gd[3]:
                    for u, c0 in enumerate(gd[2]):
                        for (d0, s0) in ((0, c0 + 64), (64, c0)):
                            em.dma("pool", dst[:, :, u * 128 + d0:u * 128 + d0 + 64],
                                   W[:, s0:s0 + 64].rearrange("(kc p) c -> p kc c", p=128), writes=[B_Wg[gi]])
                else:
                    c0 = gd[2][0]
                    ncol = 128 * len(gd[2])
                    em.dma("pool", dst[:, :, 0:ncol], W[:, c0:c0 + ncol].rearrange("(kc p) c -> p kc c", p=128),
                           writes=[B_Wg[gi]])
            elif gd[0] == "B":
                _, _, c0, k0, nk = gd
                dst = Wg[gi].rearrange("p (kc c) -> p kc c", kc=8)
                em.dma("pool", dst[:, 0:nk],
                       W[k0 * 128:(k0 + nk) * 128, c0:c0 + 512].rearrange("(kc p) c -> p kc c", p=128),
                       writes=[B_Wg[gi]])
            else:
                dst = Wg[gi, 0:64].rearrange("p (h c) -> p h c", h=8)
                c0 = gd[2][0]
                em.dma("pool", dst, W[:, c0:c0 + 512].rearrange("(h p) c -> p h c", p=64), writes=[B_Wg[gi]])

        for gi in range(6):
            cast_granule(gi)
        cast_next = [6]

        def cast_more(k):
            for _ in range(k):
                if cast_next[0] < NG:
                    cast_granule(cast_next[0])
                    cast_next[0] += 1

        em.dma("sp", ident.t[:], ident_d, writes=[ident.b])
        em.dma("sp", wn1.t[:], n1w.partition_broadcast(128), writes=[wn1.b])
        em.dma("sp", gq.t[:], qnw, writes=[gq.b])
        em.dma("sp", gk.t[:], knw, writes=[gk.b])
        em.dma("sp", gk2.t[0:64, :], knw, writes=[gk2.b])
        em.dma("sp", gk2.t[64:128, :], knw, writes=[gk2.b])
        op("pool", lambda: nc.gpsimd.memset(onesblk.t[:], 0.0), writes=[onesblk.b])
        op("pool", lambda: nc.gpsimd.memset(onesblk.t[0:64, 0:64], 1.0), writes=[onesblk.b])
        op("pool", lambda: nc.gpsimd.memset(onesblk.t[64:128, 64:128], 1.0), writes=[onesblk.b])
        op("pool", lambda: nc.gpsimd.memset(onesf.t[:], 1.0), writes=[onesf.b])
        op("pool", lambda: nc.gpsimd.memset(onesb.t[:], 1.0), writes=[onesb.b])

        seq1 = []
        for s in range(NSEG):
            if s < 8:
                seq1 += [0, 1, 3, 4, 2, 5]
            else:
                seq1 += [0, 1, 6, 7, 3, 4, 13, 14, 2, 5, 8, 9, 15, 10, 16, 11, 12]
        seqC = []
        sF = [17, 18, 19, 20]
        sG = [x for i in range(6) for x in (G_GATE + i, G_UP + i)]
        sD = list(range(G_DOWN, G_DOWN + 6))
        seqC += sF
        for j in range(8):
            seqC += sG + (sF if j + 1 < 8 else []) + sD
        ws = {"seq": seq1, "slots": wslot, "pos": 0, "emit": 0}

        def wstream(seq, slots):
            ws["seq"] = seq; ws["slots"] = slots; ws["pos"] = 0; ws["emit"] = 0

        def wget(expect):
            seq, slots = ws["seq"], ws["slots"]
            i = ws["pos"]
            assert seq[i] == expect, (i, seq[i], expect)
            ns = len(slots)
            while ws["emit"] < min(len(seq), i + ns - 1):
                k = ws["emit"]
                sl = slots[k % ns]
                gi = seq[k]
                gd = GRAN[gi]
                if gd[0] == "C":
                    em.dma("sp", sl.t[0:64, :], Wg[gi, 0:64, :], reads=[B_Wg[gi]], writes=[sl.b])
                elif gd[0] == "A" and len(gd[2]) < 4:
                    nc_ = 128 * len(gd[2])
                    em.dma("sp", sl.t[:].rearrange("p (kc c) -> p kc c", kc=8)[:, :, 0:nc_],
                           Wg[gi].rearrange("p (kc c) -> p kc c", kc=8)[:, :, 0:nc_], reads=[B_Wg[gi]], writes=[sl.b])
                elif gd[0] == "B" and gd[4] < 8:
                    em.dma("sp", sl.t[:, 0:gd[4] * 512], Wg[gi, :, 0:gd[4] * 512], reads=[B_Wg[gi]], writes=[sl.b])
                else:
                    em.dma("sp", sl.t[:], Wg[gi], reads=[B_Wg[gi]], writes=[sl.b])
                ws["emit"] += 1
            ws["pos"] += 1
            return slots[i % ns]

        vB = lambda sl: sl.t[:].rearrange("p (kc c) -> p kc c", kc=8)
        vC = lambda sl: sl.t[0:64, :].rearrange("p (h c) -> p h c", h=8)

        def mm(out_ap, lhsT, rhs, start, stop, reads, wb, inc):
            op("pe", lambda: nc.tensor.matmul(out_ap, lhsT=lhsT, rhs=rhs, start=start, stop=stop),
               reads=reads, writes=[wb], inc=inc)

        def proj_fm(sl, u, hT, M=128, c0=0):
            p = pf()
            w = vB(sl)
            for kc in range(8):
                mm(p.t[0:M, :], w[:, kc, u * 128 + c0:u * 128 + c0 + M], hT.t[:, kc, :], kc == 0, kc == 7,
                   [sl.b, hT.b], p.b, kc == 7)
            return p

        def proj_tm(sl, hT, t):
            p = pf()
            w = vB(sl)
            for kc in range(8):
                mm(p.t[:, :], hT.t[:, kc, t * 128:(t + 1) * 128], w[:, kc, :], kc == 0, kc == 7,
                   [sl.b, hT.b], p.b, kc == 7)
            return p

        def rstd_small(ss, n, cols, eps):
            op("act", lambda: nc.scalar.activation(out=ss.t[:, cols], in_=ss.t[:, cols], func=AF.Ln,
                                                   bias=eps, scale=1.0 / n), reads=[ss.b], writes=[ss.b])
            op("act", lambda: nc.scalar.activation(out=ss.t[:, cols], in_=ss.t[:, cols], func=AF.Exp,
                                                   scale=-0.5), reads=[ss.b], writes=[ss.b])

        def norm_T(xt_ap, xb, wn, ss, sqj, xs, hT, t):
            op("act", lambda: nc.scalar.activation(out=sqj.t[:], in_=xt_ap, func=AF.Square,
                                                   accum_out=ss.t[:, 0:1]),
               reads=[xb], writes=[sqj.b, ss.b])
            rstd_small(ss, D, slice(0, 1), 1e-6)
            op("dve", lambda: nc.vector.scalar_tensor_tensor(out=xs.t[:], in0=xt_ap, scalar=ss.t[:, 0:1],
                                                             in1=wn.t[:], op0=ALU.mult, op1=ALU.mult),
               reads=[xb, ss.b, wn.b], writes=[xs.b])
            pt = pb()
            for kc in range(8):
                op("pe", lambda kc=kc: nc.tensor.transpose(out=pt.t[:, kc * 128:(kc + 1) * 128],
                                                           in_=xs.t[:, kc * 128:(kc + 1) * 128],
                                                           identity=ident.t[:]),
                   reads=[xs.b, ident.b], writes=[pt.b], inc=(kc == 7))
            op("act", lambda: nc.scalar.copy(out=hT.t[:, :, t * 128:(t + 1) * 128],
                                             in_=pt.t[:].rearrange("p (k c) -> p k c", k=8)),
               reads=[pt.b], writes=[hT.b])

        with ExitStack() as stV:
            V_all = SB(stV, "V_all", [128, 64, 8, 66], BF16)
            op("pool", lambda: nc.gpsimd.memset(V_all.t[:], 1.0), writes=[V_all.b])

            with ExitStack() as st1:
                xt = [SB(st1, f"xt{i}", [128, D], F32) for i in range(2)]
                xs = SB(st1, "xs", [128, D], BF16)
                ss = SB(st1, "ss", [128, 1], F32)
                hTs = [SB(st1, f"hT{i}", [128, 8, SEG], BF16) for i in range(2)]
                ck = SB(st1, "ck", [128, SEG], F32); sk = SB(st1, "sk", [128, SEG], F32)
                tmpA = SB(st1, "tmpA", [128, SEG], F32); tmpB = SB(st1, "tmpB", [128, SEG], F32)
                rkT = SB(st1, "rkT", [128, 4, SEG], BF16)
                rqT = SB(st1, "rqT", [128, 4, SEG], BF16)
                rqxT = SB(st1, "rqxT", [128, 4, SEG], BF16)
                rvS = SB(st1, "rvS", [128, 4, D], BF16)
                sg = SB(st1, "sg", [128, 4, D], BF16)
                kz = SB(st1, "kz", [128, 512], BF16)
                STt = SB(st1, "STt", [128, 512], BF16)
                ons = [SB(st1, f"on{i}", [128, 256], F32) for i in range(2)]
                ret = SB(st1, "ret", [128, D], BF16)
                retT = SB(st1, "retT", [128, 8, SEG], BF16)
                sqm = SB(st1, "sqm", [128, SEG], BF16)
                rs = SB(st1, "rs", [128, SEG], F32)
                kTs = [SB(st1, f"kTs{i}", [128, SEG], BF16) for i in range(2)]
                ksum = SB(st1, "ksum", [128, 2], F32)
                kb2 = SB(st1, "kb2", [128, 2], BF16)
                kbarT = SB(st1, "kbarT", [64, 8, 32], BF16)
                QA = SB(st1, "QA", [96, 8, SEG], BF16)
                Z = SB(st1, "Z", [128, 8, 96], BF16)
                g3 = SB(st1, "g3", [128, 8, 32], F32)
                m3 = SB(st1, "m3", [128, 8, 32], F32)
                v3 = SB(st1, "v3", [128, 8, 32], F32)
                mx = SB(st1, "mx", [128, 8, 8], F32)
                gbt = SB(st1, "gbt", [128, 32], F32); ost = SB(st1, "ost", [128, 32], F32)
                gst = SB(st1, "gst", [128, 12], F32)
                gstb = [Buf(f"gst{i}") for i in range(9)]
                c256 = SB(st1, "c256", [128, 4], F32)
                cneg = SB(st1, "cneg", [128, 4], F32)
                op("pool", lambda: nc.gpsimd.memset(c256.t[:], 1.0 / 256), writes=[c256.b])
                op("pool", lambda: nc.gpsimd.memset(cneg.t[:], -1.0), writes=[cneg.b])
                sga = [SB(st1, f"sga{i}", [128, SEG], BF16) for i in range(2)]
                stg = [SB(st1, f"stg{i}", [128, SEG], BF16) for i in range(2)]
                Sst = SB(st1, "Sst", [128, 1024], F32)
                Sbf = SB(st1, "Sbf", [128, 1024], BF16)
                dmat = SB(st1, "dmat", [128, 512], F32)
                xit = SB(st1, "xit", [128, 512], F32)
                zet = SB(st1, "zet", [128, 512], F32)
                for tb, src in ((dmat, dmat_d), (xit, xi_d), (zet, zeta_d)):
                    em.dma("sp", tb.t[:], src, writes=[tb.b])
                op("pool", lambda: nc.gpsimd.memset(Sst.t[:], 0.0), writes=[Sst.b])
                op("pool", lambda: nc.gpsimd.memset(Sbf.t[:], 0.0), writes=[Sbf.b])
                op("pool", lambda: nc.gpsimd.memset(kbarT.t[:], 0.0), writes=[kbarT.b])
                op("pool", lambda: nc.gpsimd.memset(kb2.t[:], 0.0), writes=[kb2.b])
                op("pool", lambda: nc.gpsimd.memset(Z.t[:], 0.0), writes=[Z.b])
                xload = [0]

                def load_x(tile_idx):
                    tb = xt[tile_idx % 2]
                    em.dma("sp", tb.t[:], xv[tile_idx * 128:(tile_idx + 1) * 128, :], writes=[tb.b])

                load_x(0)

                def emit_norm(s_):
                    for t in range(4):
                        ti = s_ * 4 + t
                        if ti + 1 < NSEG * 4:
                            load_x(ti + 1)
                        norm_T(xt[ti % 2].t[:], xt[ti % 2].b, wn1, ss, xs, xs, hTs[s_ % 2], t)

                def rotary(p1, p2, ct, st_, dst, h):
                    op("dve", lambda: nc.vector.tensor_tensor(out=tmpA.t[:], in0=p1.t[:], in1=ct.t[:], op=ALU.mult),
                       reads=[p1.b, ct.b], writes=[tmpA.b])
                    op("dve", lambda: nc.vector.tensor_tensor(out=tmpB.t[:], in0=p2.t[:], in1=st_.t[:], op=ALU.mult),
                       reads=[p2.b, st_.b], writes=[tmpB.b])
                    op("dve", lambda: nc.vector.tensor_tensor(out=dst.t[:, h, :], in0=tmpA.t[:], in1=tmpB.t[:],
                                                              op=ALU.add),
                       reads=[tmpA.b, tmpB.b], writes=[dst.b])

                def qknorm(p, gw, dst_ap, dst_b):
                    op("act", lambda: nc.scalar.activation(out=sqm.t[0:64, :], in_=p.t[0:64, :], func=AF.Square),
                       reads=[p.b], writes=[sqm.b])
                    p2 = pf()
                    mm(p2.t[0:64, :], onesb.t[:], sqm.t[0:64, :], True, True, [onesb.b, sqm.b], p2.b, True)
                    op("act", lambda: nc.scalar.activation(out=rs.t[0:64, :], in_=p2.t[0:64, :], func=AF.Ln,
                                                           bias=1e-6, scale=1.0 / 64),
                       reads=[p2.b], writes=[rs.b])
                    op("act", lambda: nc.scalar.activation(out=rs.t[0:64, :], in_=rs.t[0:64, :], func=AF.Exp, scale=-0.5),
                       reads=[rs.b], writes=[rs.b])
                    op("dve", lambda: nc.vector.scalar_tensor_tensor(out=dst_ap, in0=p.t[0:64, :], scalar=gw.t[:, 0:1],
                                                                     in1=rs.t[0:64, :], op0=ALU.mult, op1=ALU.mult),
                       reads=[p.b, gw.b, rs.b], writes=[dst_b])

                def normA(s_, t):
                    ti = s_ * 4 + t
                    if ti + 1 < NSEG * 4:
                        load_x(ti + 1)
                    xtb = xt[ti % 2]
                    op("act", lambda: nc.scalar.activation(out=xs.t[:], in_=xtb.t[:], func=AF.Square,
                                                           accum_out=ss.t[:, 0:1]),
                       reads=[xtb.b], writes=[xs.b, ss.b])
                    rstd_small(ss, D, slice(0, 1), 1e-6)
                    op("dve", lambda: nc.vector.scalar_tensor_tensor(out=xs.t[:], in0=xtb.t[:], scalar=ss.t[:, 0:1],
                                                                     in1=wn1.t[:], op0=ALU.mult, op1=ALU.mult),
                       reads=[xtb.b, ss.b, wn1.b], writes=[xs.b])

                def normB(s_, t):
                    pt = pb()
                    for kc in range(8):
                        op("pe", lambda kc=kc: nc.tensor.transpose(out=pt.t[:, kc * 128:(kc + 1) * 128],
                                                                   in_=xs.t[:, kc * 128:(kc + 1) * 128],
                                                                   identity=ident.t[:]),
                           reads=[xs.b, ident.b], writes=[pt.b], inc=(kc == 7))
                    op("act", lambda: nc.scalar.copy(out=hTs[s_ % 2].t[:, :, t * 128:(t + 1) * 128],
                                                     in_=pt.t[:].rearrange("p (k c) -> p k c", k=8)),
                       reads=[pt.b], writes=[hTs[s_ % 2].b])

                emit_norm(0)
                for s in range(NSEG):
                    own = s >= 8
                    j = s - 8
                    cs = slice(s * SEG, (s + 1) * SEG)
                    cast_more(3)
                    em.dma("sp", ck.t[:], cosk_d[:, cs], writes=[ck.b])
                    em.dma("sp", sk.t[:], sink_d[:, cs], writes=[sk.b])
                    hT = hTs[s % 2]
                    s0 = wget(0); s1 = wget(1)
                    for h in range(4):
                        rotary(proj_fm(s0, h, hT), proj_fm(s1, h, hT), ck, sk, rkT, h)
                    if own:
                        em.dma("sp", ck.t[:], cosq_d[:, j * SEG:(j + 1) * SEG], writes=[ck.b])
                        em.dma("sp", sk.t[:], sinq_d[:, j * SEG:(j + 1) * SEG], writes=[sk.b])
                        s0 = wget(6); s1 = wget(7)
                        for h in range(4):
                            rotary(proj_fm(s0, h, hT), proj_fm(s1, h, hT), ck, sk, rqT, h)
                            op("pool", lambda h=h: nc.gpsimd.tensor_tensor(
                                out=rqxT.t[:, h, :].rearrange("p (c n) -> p c n", c=4),
                                in0=rqT.t[:, h, :].rearrange("p (c n) -> p c n", c=4),
                                in1=xit.t[:, None, h * 128:(h + 1) * 128].to_broadcast([128, 4, 128]), op=ALU.mult),
                               reads=[rqT.b, xit.b], writes=[rqxT.b])
                    sv = [wget(3), wget(4)]
                    for t in range(4):
                        for hf in range(2):
                            p = proj_tm(sv[hf], hT, t)
                            op("act", lambda p=p, t=t, hf=hf: nc.scalar.copy(
                                out=rvS.t[:, t, hf * 512:(hf + 1) * 512], in_=p.t[:]),
                               reads=[p.b], writes=[rvS.b])
                    if own:
                        sgw = [wget(13), wget(14)]
                        for t in range(4):
                            for hf in range(2):
                                p = proj_tm(sgw[hf], hT, t)
                                op("act", lambda p=p, t=t, hf=hf: nc.scalar.activation(
                                    out=sg.t[:, t, hf * 512:(hf + 1) * 512], in_=p.t[:], func=AF.Silu),
                                   reads=[p.b], writes=[sg.b])
                    pos_all = {}

                    def chunk_s1(c):
                        cc = slice(c * 128, (c + 1) * 128)
                        pos_ = []
                        pos_all[c] = pos_
                        if own:
                            psc = pf()
                            for h in range(4):
                                mm(psc.t[:, h * 128:(h + 1) * 128], rkT.t[:, h, cc], rqT.t[:, h, cc], True, True,
                                   [rkT.b, rqT.b], psc.b, h == 3)
                            op("dve", lambda psc=psc: nc.vector.tensor_tensor(out=STt.t[:], in0=psc.t[:], in1=dmat.t[:],
                                                                             op=ALU.mult),
                               reads=[psc.b, dmat.b], writes=[STt.b])
                            for hp in range(2):
                                po = psF[4 + hp]
                                pos_.append(po)
                                for hh in range(2):
                                    h = hp * 2 + hh
                                    mm(po.t[:, hh * 256:(hh + 1) * 256], STt.t[:, h * 128:(h + 1) * 128],
                                       rvS.t[:, c, h * 256:(h + 1) * 256], True, False, [STt.b, rvS.b], po.b, False)
                                    mm(po.t[:, hh * 256:(hh + 1) * 256], rqxT.t[:, h, cc],
                                       Sbf.t[:, h * 256:(h + 1) * 256], False, True, [rqxT.b, Sbf.b], po.b, hh == 1)
                        pt = pb()
                        for h in range(4):
                            op("pe", lambda h=h, pt=pt, cc=cc: nc.tensor.transpose(
                                out=pt.t[:, h * 128:(h + 1) * 128], in_=rkT.t[:, h, cc], identity=ident.t[:]),
                               reads=[rkT.b, ident.b], writes=[pt.b], inc=(h == 3))
                        op("dve", lambda pt=pt: nc.vector.tensor_tensor(out=kz.t[:], in0=pt.t[:, 0:512], in1=zet.t[:],
                                                                       op=ALU.mult),
                           reads=[pt.b, zet.b], writes=[kz.b])
                        for hp in range(2):
                            pu = pf()
                            for hh in range(2):
                                h = hp * 2 + hh
                                mm(pu.t[:, hh * 256:(hh + 1) * 256], kz.t[:, h * 128:(h + 1) * 128],
                                   rvS.t[:, c, h * 256:(h + 1) * 256], True, True, [kz.b, rvS.b], pu.b, hh == 1)
                            for hh in range(2):
                                h = hp * 2 + hh
                                hs = slice(h * 256, (h + 1) * 256)
                                op("dve", lambda hs=hs, pu=pu, hh=hh, h=h: nc.vector.scalar_tensor_tensor(
                                    out=Sst.t[:, hs], in0=Sst.t[:, hs], scalar=float(G_CHUNK[h]),
                                    in1=pu.t[:, hh * 256:(hh + 1) * 256], op0=ALU.mult, op1=ALU.add),
                                   reads=[Sst.b, pu.b], writes=[Sst.b])
                        op("dve", lambda: nc.vector.tensor_copy(out=Sbf.t[:], in_=Sst.t[:]), reads=[Sst.b], writes=[Sbf.b])

                    def chunk_s2(c):
                        cc = slice(c * 128, (c + 1) * 128)
                        pos_ = pos_all[c]
                        if own:
                            for h in range(4):
                                oap = pos_[h // 2].t[:, (h % 2) * 256:(h % 2 + 1) * 256]
                                op("act", lambda oap=oap, h=h: nc.scalar.activation(
                                    out=ret.t[:, h * 256:(h + 1) * 256], in_=oap, func=AF.Identity,
                                    accum_out=gst.t[:, h:h + 1]), reads=[pos_[h // 2].b], writes=[gstb[h], ret.b])
                                op("act", lambda oap=oap, h=h: nc.scalar.activation(
                                    out=ret.t[:, h * 256:(h + 1) * 256], in_=oap, func=AF.Square,
                                    accum_out=gst.t[:, 4 + h:5 + h]), reads=[pos_[h // 2].b], writes=[gstb[4 + h], ret.b])
                            TT = nc.gpsimd.tensor_tensor
                            op("pool", lambda: TT(out=gst.t[:, 0:4], in0=gst.t[:, 0:4], in1=c256.t[:], op=ALU.mult),
                               reads=gstb[0:4] + [c256.b], writes=gstb[0:4])
                            op("pool", lambda: TT(out=gst.t[:, 8:12], in0=gst.t[:, 0:4], in1=gst.t[:, 0:4], op=ALU.mult),
                               reads=gstb[0:4], writes=[gstb[8]])
                            op("pool", lambda: TT(out=gst.t[:, 4:8], in0=gst.t[:, 4:8], in1=c256.t[:], op=ALU.mult),
                               reads=gstb[4:8] + [c256.b], writes=gstb[4:8])
                            op("pool", lambda: TT(out=gst.t[:, 4:8], in0=gst.t[:, 4:8], in1=gst.t[:, 8:12], op=ALU.subtract),
                               reads=gstb[4:9], writes=gstb[4:8])
                            op("act", lambda: nc.scalar.activation(out=gst.t[:, 4:8], in_=gst.t[:, 4:8], func=AF.Ln,
                                                                   bias=1e-5, scale=1.0),
                               reads=gstb[4:8], writes=gstb[4:8])
                            op("act", lambda: nc.scalar.activation(out=gst.t[:, 4:8], in_=gst.t[:, 4:8], func=AF.Exp,
                                                                   scale=-0.5),
                               reads=gstb[4:8], writes=gstb[4:8])
                            op("pool", lambda: TT(out=gst.t[:, 8:12], in0=gst.t[:, 0:4], in1=cneg.t[:], op=ALU.mult),
                               reads=gstb[0:4] + [cneg.b], writes=[gstb[8]])
                            op("pool", lambda: TT(out=gst.t[:, 8:12], in0=gst.t[:, 8:12], in1=gst.t[:, 4:8], op=ALU.mult),
                               reads=gstb[4:9], writes=[gstb[8]])
                            for h in range(4):
                                oap = pos_[h // 2].t[:, (h % 2) * 256:(h % 2 + 1) * 256]
                                onb = ons[h % 2]
                                op("act", lambda oap=oap, h=h, onb=onb: nc.scalar.activation(
                                    out=onb.t[:], in_=oap, func=AF.Identity, scale=gst.t[:, 4 + h:5 + h],
                                    bias=gst.t[:, 8 + h:9 + h]), reads=[pos_[h // 2].b] + gstb[0:9], writes=[onb.b])
                                op("pool", lambda h=h, c=c, onb=onb: TT(
                                    out=ret.t[:, h * 256:(h + 1) * 256], in0=onb.t[:],
                                    in1=sg.t[:, c, h * 256:(h + 1) * 256], op=ALU.mult),
                                   reads=[onb.b, sg.b], writes=[ret.b])
                            pt = pb()
                            for kc in range(8):
                                op("pe", lambda kc=kc, pt=pt: nc.tensor.transpose(
                                    out=pt.t[:, kc * 128:(kc + 1) * 128], in_=ret.t[:, kc * 128:(kc + 1) * 128],
                                    identity=ident.t[:]), reads=[ret.b, ident.b], writes=[pt.b], inc=(kc == 7))
                            op("act", lambda pt=pt, cc=cc: nc.scalar.copy(
                                out=retT.t[:, :, cc], in_=pt.t[:].rearrange("p (k c) -> p k c", k=8)),
                               reads=[pt.b], writes=[retT.b])

                    def mk_pair(u, s2_):
                        p = proj_fm(s2_, u, hT)
                        kst = kTs[u % 2]
                        op("act", lambda: nc.scalar.activation(out=sqm.t[:], in_=p.t[:], func=AF.Square),
                           reads=[p.b], writes=[sqm.b])
                        p2 = pf()
                        mm(p2.t[:], onesblk.t[:], sqm.t[:], True, True, [onesblk.b, sqm.b], p2.b, True)
                        op("act", lambda: nc.scalar.activation(out=rs.t[:], in_=p2.t[:], func=AF.Ln, bias=1e-6,
                                                               scale=1.0 / 64), reads=[p2.b], writes=[rs.b])
                        op("act", lambda: nc.scalar.activation(out=rs.t[:], in_=rs.t[:], func=AF.Exp, scale=-0.5),
                           reads=[rs.b], writes=[rs.b])
                        op("dve", lambda: nc.vector.scalar_tensor_tensor(out=kst.t[:], in0=p.t[:], scalar=gk2.t[:, 0:1],
                                                                         in1=rs.t[:], op0=ALU.mult, op1=ALU.mult),
                           reads=[p.b, gk2.b, rs.b], writes=[kst.b])
                        op("dve", lambda: nc.vector.reduce_sum(
                            out=ksum.t[:], in_=kst.t[:].rearrange("p (b k) -> p b k", b=2), axis=AX.X),
                           reads=[kst.b], writes=[ksum.b])
                        op("dve", lambda: nc.vector.tensor_scalar(
                            out=kbarT.t[:, 2 * u, 2 * s:2 * s + 2], in0=ksum.t[0:64, :], scalar1=1.0 / 256, scalar2=None,
                            op0=ALU.mult), reads=[ksum.b], writes=[kbarT.b])
                        op("dve", lambda: nc.vector.tensor_scalar(
                            out=kb2.t[64:128, :], in0=ksum.t[64:128, :], scalar1=1.0 / 256, scalar2=None,
                            op0=ALU.mult), reads=[ksum.b], writes=[kb2.b])
                        pk = pf()
                        mm(pk.t[0:64, 0:2], ident.t[:, 64:128], kb2.t[:], True, True, [ident.b, kb2.b], pk.b, True)
                        op("act", lambda: nc.scalar.copy(out=kbarT.t[:, 2 * u + 1, 2 * s:2 * s + 2], in_=pk.t[0:64, 0:2]),
                           reads=[pk.b], writes=[kbarT.b])
                        em.dma("pool", KT[2 * u:2 * u + 2, :, cs].rearrange("h p c -> (h p) c"), kst.t[:],
                               reads=[kst.b], writes=[B_KT[2 * u], B_KT[2 * u + 1]])

                    def mv_step(s5_):
                        for t in range(4):
                            p = proj_tm(s5_, hT, t)
                            op("act", lambda p=p, t=t, s=s: nc.scalar.copy(
                                out=V_all.t[:, s * 4 + t, :, 0:64], in_=p.t[:].rearrange("p (h d) -> p h d", h=8)),
                               reads=[p.b], writes=[V_all.b])

                    def mq_step(h, s8_):
                        u, e = h // 2, h % 2
                        p = proj_fm(s8_, u, hT, M=64, c0=64 * e)
                        qknorm(p, gq, QA.t[0:64, h, :], QA.b)

                    def gate_s1(t):
                        qs = slice(j * SEG + t * 128, j * SEG + (t + 1) * 128)
                        em.dma("sp", gbt.t[:], gbias_d[qs, :], writes=[gbt.b])
                        em.dma("sp", ost.t[:], ownsel_d[qs, :], writes=[ost.b])
                        pg = pf()
                        for h in range(8):
                            mm(pg.t[:, h * 32:(h + 1) * 32], QA.t[0:64, h, t * 128:(t + 1) * 128], kbarT.t[:, h, :],
                               True, True, [QA.b, kbarT.b], pg.b, h == 7)
                        op("dve", lambda pg=pg: nc.vector.tensor_tensor(
                            out=g3.t[:], in0=pg.t[:, 0:256].rearrange("p (h n) -> p h n", h=8),
                            in1=gbt.t[:, None, :].to_broadcast([128, 8, 32]), op=ALU.add),
                           reads=[pg.b, gbt.b], writes=[g3.b])
                        for h in range(8):
                            op("dve", lambda h=h: nc.vector.max(out=mx.t[:, h, :], in_=g3.t[:, h, :]),
                               reads=[g3.b], writes=[mx.b])
                        op("dve", lambda: nc.vector.tensor_tensor(
                            out=m3.t[:], in0=g3.t[:], in1=mx.t[:, :, 2:3].to_broadcast([128, 8, 32]), op=ALU.is_ge),
                           reads=[g3.b, mx.b], writes=[m3.b])
                        op("dve", lambda: nc.vector.tensor_single_scalar(out=v3.t[:], in_=g3.t[:], scalar=-1e29,
                                                                         op=ALU.is_gt),
                           reads=[g3.b], writes=[v3.b])
                        op("dve", lambda: nc.vector.tensor_tensor(out=m3.t[:], in0=m3.t[:], in1=v3.t[:], op=ALU.mult),
                           reads=[m3.b, v3.b], writes=[m3.b])
                        op("dve", lambda: nc.vector.tensor_tensor(
                            out=m3.t[:], in0=m3.t[:], in1=ost.t[:, None, :].to_broadcast([128, 8, 32]), op=ALU.add),
                           reads=[m3.b, ost.b], writes=[m3.b])
                        op("dve", lambda: nc.vector.tensor_scalar(
                            out=Z.t[:, :, 64:96], in0=m3.t[:], scalar1=-1.0, scalar2=NEGB, op0=ALU.add, op1=ALU.mult),
                           reads=[m3.b], writes=[Z.b])

                    def gate_s2(t):
                        pt = pb()
                        for h in range(8):
                            op("pe", lambda h=h, pt=pt: nc.tensor.transpose(
                                out=pt.t[0:96, h * 128:(h + 1) * 128], in_=Z.t[:, h, :], identity=ident.t[:]),
                               reads=[Z.b, ident.b], writes=[pt.b], inc=(h == 7))
                        op("act", lambda pt=pt, t=t: nc.scalar.copy(
                            out=QA.t[64:96, :, t * 128:(t + 1) * 128],
                            in_=pt.t[64:96, :].rearrange("p (h c) -> p h c", h=8)),
                           reads=[pt.b], writes=[QA.b])

                    def fin_a(sa, sr, half, u):
                        uu = half * 4 + u
                        pga = proj_fm(sa, u, hT)
                        sgt = sga[uu % 2]
                        op("act", lambda pga=pga, sgt=sgt: nc.scalar.activation(out=sgt.t[:], in_=pga.t[:],
                                                                                func=AF.Sigmoid),
                           reads=[pga.b], writes=[sgt.b])
                        pa = proj_fm(sr, u, retT)
                        sg_ = stg[uu % 2]
                        op("dve", lambda pa=pa, sgt=sgt, sg_=sg_: nc.vector.tensor_tensor(
                            out=sg_.t[:], in0=pa.t[:], in1=sgt.t[:], op=ALU.mult),
                           reads=[pa.b, sgt.b], writes=[sg_.b])
                        em.dma("pool", MIXA[:, uu, j * SEG:(j + 1) * SEG], sg_.t[:], reads=[sg_.b], writes=[B_MIXA])

                    def fin_b(sb_, half, u):
                        uu = half * 4 + u
                        pgb = proj_fm(sb_, u, hT)
                        sgt = sga[uu % 2]
                        op("act", lambda pgb=pgb, sgt=sgt: nc.scalar.activation(out=sgt.t[:], in_=pgb.t[:],
                                                                                func=AF.Sigmoid),
                           reads=[pgb.b], writes=[sgt.b])
                        em.dma("pool", SGB[:, uu, j * SEG:(j + 1) * SEG], sgt.t[:], reads=[sgt.b], writes=[B_SGB])

                    s2_ = wget(2)
                    N_ = s + 1 < NSEG

                    def nA(t):
                        if N_:
                            normA(s + 1, t)

                    def nB(t):
                        if N_:
                            normB(s + 1, t)

                    nA(0); chunk_s1(0); mk_pair(0, s2_); nB(0); nA(1)
                    if own:
                        chunk_s2(0)
                    mk_pair(1, s2_); nB(1); nA(2)
                    chunk_s1(1); mk_pair(2, s2_); nB(2); nA(3)
                    if own:
                        chunk_s2(1)
                    mk_pair(3, s2_); nB(3)
                    s5_ = wget(5)
                    chunk_s1(2); mv_step(s5_)
                    if own:
                        chunk_s2(2)
                    if not own:
                        chunk_s1(3)
                        continue
                    s8_ = wget(8)
                    mq_step(0, s8_); mq_step(1, s8_)
                    chunk_s1(3); mq_step(2, s8_); mq_step(3, s8_)
                    chunk_s2(3)
                    for h in range(4, 8):
                        mq_step(h, s8_)
                    sa = wget(9); sr = wget(15)
                    gate_s1(0); fin_a(sa, sr, 0, 0); fin_a(sa, sr, 0, 1)
                    gate_s2(0); gate_s1(1); fin_a(sa, sr, 0, 2); fin_a(sa, sr, 0, 3)
                    sa = wget(10); sr = wget(16)
                    gate_s2(1); gate_s1(2); fin_a(sa, sr, 1, 0); fin_a(sa, sr, 1, 1)
                    gate_s2(2); gate_s1(3); fin_a(sa, sr, 1, 2); fin_a(sa, sr, 1, 3)
                    gate_s2(3)
                    for h in range(8):
                        em.dma("pool", QS[h, :, j * SEG:(j + 1) * SEG], QA.t[:, h, :], reads=[QA.b], writes=[B_QS[h]])
                    for half in range(2):
                        sb_ = wget(11 + half)
                        for u in range(4):
                            fin_b(sb_, half, u)
                cast_more(NG)
                em.barrier()

            with ExitStack() as stB:
                KA = [SB(stB, f"KA{i}", [96, SV], BF16) for i in range(2)]
                QB = [SB(stB, f"QB{i}", [96, S_HALF], BF16) for i in range(2)]
                ptile = [SB(stB, f"ptile{i}", [128, 2 * SEG], BF16) for i in range(3)]
                cm = SB(stB, "cm", [128, 512], BF16)
                rl = SB(stB, "rl", [128, SEG], F32)
                osb = SB(stB, "osb", [64, SEG], F32)
                ot = [SB(stB, f"ot{i}", [64, SEG], BF16) for i in range(2)]
                em.dma("sp", cm.t[:], cm_d, writes=[cm.b])
                for i in range(2):
                    em.dma("sp", KA[i].t[64:96, :], eoh_d, writes=[KA[i].b])
                psO = [psF[4], psF[5]]
                it = 0
                sp_cnt = [0]
                pend_epi = []
                pbc = TB(psB[0].t[:].bitcast(F32), "pbc")
                def loadKQ(h_):
                    em.dma("sp", KA[h_ % 2].t[0:64, :], KT[h_], reads=[B_KT[h_]], writes=[KA[h_ % 2].b])
                    em.dma("sp", QB[h_ % 2].t[:], QS[h_], reads=[B_QS[h_]], writes=[QB[h_ % 2].b])

                loadKQ(0)
                for h in range(8):
                    ka = KA[h % 2]; qb = QB[h % 2]
                    if h + 1 < 8:
                        loadKQ(h + 1)
                    for j in range(8):
                        nkt = 4 * (8 + j + 1)
                        po = psO[it % 2]
                        qsl = slice(j * SEG, (j + 1) * SEG)
                        pi = 0

                        def s_mm(kt):
                            ps_ = pf()
                            mm(ps_.t[:], ka.t[:, kt * 128:(kt + 1) * 128], qb.t[:, qsl], True, True,
                               [ka.b, qb.b], ps_.b, True)
                            return ps_

                        ps_q = [s_mm(k_) for k_ in range(min(2, nkt))]
                        while pend_epi:
                            pend_epi.pop(0)()
                        for kt in range(nkt):
                            ps_cur = ps_q.pop(0)
                            if kt + 2 < nkt:
                                ps_q.append(s_mm(kt + 2))
                            ptb = ptile[pi % 3]; pi += 1
                            op("act", lambda ps_cur=ps_cur, ptb=ptb: nc.scalar.activation(
                                out=ptb.t[:, 0:SEG], in_=ps_cur.t[:], func=AF.Exp, scale=0.125),
                               reads=[ps_cur.b], writes=[ptb.b])
                            dk = kt - (nkt - 4)
                            if dk >= 0:
                                q0 = 0 if dk < 2 else 256
                                mcol = (dk % 2) * 256
                                op("dve", lambda ptb=ptb, q0=q0, mcol=mcol: nc.vector.tensor_tensor(
                                    out=ptb.t[:, q0:q0 + 256], in0=ptb.t[:, q0:q0 + 256],
                                    in1=cm.t[:, mcol:mcol + 256], op=ALU.mult),
                                   reads=[ptb.b, cm.b], writes=[ptb.b])
                            mm(po.t[0:66, :], V_all.t[:, kt, h, :], ptb.t[:, 0:SEG], kt == 0, kt == nkt - 1,
                               [V_all.b, ptb.b], po.b, kt == nkt - 1)
                        def epilogue(po=po, h=h, qsl=qsl, otb=ot[it % 2]):
                            op("act", lambda: nc.scalar.activation(out=rl.t[64:65, :], in_=po.t[64:65, :], func=AF.Ln),
                               reads=[po.b], writes=[rl.b])
                            op("act", lambda: nc.scalar.activation(out=rl.t[64:65, :], in_=rl.t[64:65, :], func=AF.Exp,
                                                                   scale=-1.0), reads=[rl.b], writes=[rl.b])
                            mm(pbc.t[0:64, :], onesf.t[64:65, 0:64], rl.t[64:65, :], True, True, [onesf.b, rl.b], pbc.b, True)
                            op("dve", lambda: nc.vector.tensor_copy(out=osb.t[:], in_=pbc.t[0:64, :]),
                               reads=[pbc.b], writes=[osb.b])
                            op("dve", lambda: nc.vector.tensor_tensor(
                                out=otb.t[:], in0=po.t[0:64, :], in1=osb.t[:], op=ALU.mult),
                               reads=[osb.b, po.b], writes=[otb.b])
                            em.dma("pool", OT[h, :, qsl], otb.t[:], reads=[otb.b], writes=[B_OT])

                        pend_epi.append(epilogue)
                        it += 1
                while pend_epi:
                    pend_epi.pop(0)()
                em.barrier()

        with ExitStack() as stC:
            wstream(seqC, wslot + [SB(stC, f"wslotC{i}", [128, 4096], BF16) for i in range(5)])
            wn2 = SB(stC, "wn2", [128, D], F32)
            em.dma("sp", wn2.t[:], n2w.partition_broadcast(128), writes=[wn2.b])
            x1s = [SB(stC, f"x1_{i}", [128, 4, D], F32) for i in range(2)]
            otls = [SB(stC, f"otl{i}", [64, 8, SEG], BF16) for i in range(2)]
            mxas = [SB(stC, f"mxa{i}", [128, 8, SEG], BF16) for i in range(2)]
            sgbls = [SB(stC, f"sgbl{i}", [128, 8, SEG], BF16) for i in range(2)]
            mixT = SB(stC, "mixT", [128, 8, SEG], BF16)
            tmpc = SB(stC, "tmpc", [128, SEG], F32)
            h2T = SB(stC, "h2T", [128, 8, SEG], BF16)
            actT = SB(stC, "actT", [128, NKC_F, SEG], BF16)
            sil = [SB(stC, f"sil{i}", [128, SEG], BF16) for i in range(2)]
            sqj = SB(stC, "sqjc", [128, D], BF16)
            xs = SB(stC, "xsc", [128, D], BF16)
            ss = SB(stC, "ssc", [128, 1], F32)
            yo = [SB(stC, f"yo{i}", [128, 512], F32) for i in range(2)]
            yic = [0]

            def loadC(j):
                qsl = slice(j * SEG, (j + 1) * SEG)
                x1 = x1s[j % 2]; otl = otls[j % 2]; mxa = mxas[j % 2]; sgbl = sgbls[j % 2]
                em.dma("sp", x1.t[:], xv[S_HALF + j * SEG:S_HALF + (j + 1) * SEG, :].rearrange("(t p) d -> p t d", p=128),
                       writes=[x1.b])
                em.dma("pool", otl.t[:], OT[:, :, qsl].rearrange("h p c -> p h c"), reads=[B_OT], writes=[otl.b])
                em.dma("pool", mxa.t[:], MIXA[:, :, qsl], reads=[B_MIXA], writes=[mxa.b])
                em.dma("pool", sgbl.t[:], SGB[:, :, qsl], reads=[B_SGB], writes=[sgbl.b])

            loadC(0)

            def front(j):
                x1 = x1s[j % 2]; otl = otls[j % 2]; mxa = mxas[j % 2]; sgbl = sgbls[j % 2]
                for half in range(2):
                    sm = wget(17 + half)
                    w = vC(sm)
                    for u in range(4):
                        uu = half * 4 + u
                        p = pf()
                        for h in range(8):
                            mm(p.t[:], w[:, h, u * 128:(u + 1) * 128], otl.t[:, h, :], h == 0, h == 7, [sm.b, otl.b], p.b, h == 7)
                        op("dve", lambda p=p, uu=uu: nc.vector.tensor_tensor(out=tmpc.t[:], in0=p.t[:],
                                                                            in1=sgbl.t[:, uu, :], op=ALU.mult),
                           reads=[p.b, sgbl.b], writes=[tmpc.b])
                        op("pool", lambda uu=uu: nc.gpsimd.tensor_tensor(out=mixT.t[:, uu, :], in0=tmpc.t[:],
                                                                         in1=mxa.t[:, uu, :], op=ALU.add),
                           reads=[tmpc.b, mxa.b], writes=[mixT.b])
                so = [wget(19), wget(20)]
                for t in range(4):
                    for hf in range(2):
                        p = pf()
                        w = vB(so[hf])
                        for kc in range(8):
                            mm(p.t[:], mixT.t[:, kc, t * 128:(t + 1) * 128], w[:, kc, :], kc == 0, kc == 7,
                               [so[hf].b, mixT.b], p.b, kc == 7)
                        op("dve", lambda p=p, t=t, hf=hf: nc.vector.tensor_tensor(
                            out=x1.t[:, t, hf * 512:(hf + 1) * 512], in0=x1.t[:, t, hf * 512:(hf + 1) * 512],
                            in1=p.t[:], op=ALU.add), reads=[x1.b, p.b], writes=[x1.b])
                for t in range(4):
                    norm_T(x1.t[:, t, :], x1.b, wn2, ss, sqj, xs, h2T, t)

            def gateup(j):
                for i in range(6):
                    sgt_ = wget(G_GATE + i)
                    sup = wget(G_UP + i)
                    for u in range(min(4, NKC_F - 4 * i)):
                        uu = 4 * i + u
                        pg_ = proj_fm(sgt_, u, h2T)
                        sl_ = sil[uu % 2]
                        op("act", lambda pg_=pg_, sl_=sl_: nc.scalar.activation(out=sl_.t[:], in_=pg_.t[:], func=AF.Silu),
                           reads=[pg_.b], writes=[sl_.b])
                        pu_ = proj_fm(sup, u, h2T)
                        op("dve", lambda pu_=pu_, sl_=sl_, uu=uu: nc.vector.tensor_tensor(
                            out=actT.t[:, uu, :], in0=pu_.t[:], in1=sl_.t[:], op=ALU.mult),
                           reads=[pu_.b, sl_.b], writes=[actT.b])

            def down(j):
                x1 = x1s[j % 2]; otl = otls[j % 2]; mxa = mxas[j % 2]; sgbl = sgbls[j % 2]
                for hf in range(2):
                    for k in range(3):
                        sl_ = wget(G_DOWN + hf * 3 + k)
                        nk = 8 if k < 2 else NKC_F - 16
                        for t in range(4):
                            p = psF[t]
                            for kk in range(nk):
                                kc = k * 8 + kk
                                mm(p.t[:], actT.t[:, kc, t * 128:(t + 1) * 128], vB(sl_)[:, kk, :], kc == 0,
                                   kc == NKC_F - 1, [sl_.b, actT.b], p.b, kk == nk - 1)
                    for t in range(4):
                        p = psF[t]
                        y = yo[yic[0] % 2]; yic[0] += 1
                        op("dve", lambda p=p, t=t, hf=hf, y=y: nc.vector.tensor_tensor(
                            out=y.t[:], in0=x1.t[:, t, hf * 512:(hf + 1) * 512], in1=p.t[:], op=ALU.add),
                           reads=[x1.b, p.b], writes=[y.b])
                        em.dma("pool", out_d[j * SEG + t * 128:j * SEG + (t + 1) * 128, hf * 512:(hf + 1) * 512],
                               y.t[:], reads=[y.b], writes=[B_out])

            front(0)
            for j in range(8):
                if j + 1 < 8:
                    loadC(j + 1)
                gateup(j)
                if j + 1 < 8:
                    front(j + 1)
                down(j)
            em.barrier()
    return nc


def _const_tables(parity):
    bf = ml_dtypes.bfloat16
    half = 64
    inv = (np.float32(10000.0) ** (-np.arange(half, dtype=np.float32) / np.float32(half))).astype(np.float32)
    pos = np.arange(SV, dtype=np.float32)
    if parity == 0:
        pos = np.where(pos >= S_HALF, pos - S_HALF, 0.0).astype(np.float32)
    ang = (pos[:, None] * inv[None, :]).astype(np.float32)
    cos = np.cos(ang).astype(np.float32).T
    sin = np.sin(ang).astype(np.float32).T
    cosT = np.concatenate([cos, cos], 0)
    sinT = np.concatenate([-sin, sin], 0)
    sc = np.float32(128.0 ** -0.5)
    t = {"cosk": cosT * sc, "sink": sinT * sc,
         "cosq": np.ascontiguousarray(cosT[:, S_HALF:]), "sinq": np.ascontiguousarray(sinT[:, S_HALF:])}
    q = np.arange(S_HALF)
    vb = 16 + q // 256
    n = np.arange(32)
    valid = n[None, :] < vb[:, None]
    if parity == 0:
        valid &= n[None, :] >= 16
    t["gbias"] = np.where(valid, 0.0, -1e30).astype(np.float32)
    t["ownsel"] = (n[None, :] == vb[:, None]).astype(np.float32)
    lg = np.log1p(-np.exp2(-5.0 - np.arange(4, dtype=np.float32))).astype(np.float32)
    idx = np.arange(128, dtype=np.float32)
    diff = idx[None, :] - idx[:, None]
    dmT = np.where(diff >= 0, np.exp(np.maximum(diff, 0)[None] * lg[:, None, None]), 0.0).astype(np.float32)
    t["dmatT"] = np.ascontiguousarray(dmT.transpose(1, 0, 2).reshape(128, 512))
    xi = np.exp((idx + 1.0)[None, :] * lg[:, None]).astype(np.float32)
    t["xitab"] = np.ascontiguousarray(np.broadcast_to(xi.reshape(1, 512), (128, 512))).astype(np.float32)
    zeta = np.exp((127.0 - idx)[None, :] * lg[:, None]).astype(np.float32)
    t["zetatab"] = np.ascontiguousarray(np.repeat(zeta.T[:, :, None], 128, 2).reshape(128, 512))
    t["ident"] = np.eye(128, dtype=np.float32).astype(bf)
    m = np.arange(128)[:, None]; nn = np.arange(256)[None, :]
    t["cmask"] = np.concatenate([(m <= nn), (m + 128 <= nn)], 1).astype(np.float32).astype(bf)
    t["eonehot"] = (np.arange(SV)[None, :] // 256 == np.arange(32)[:, None]).astype(np.float32).astype(bf)
    return t, xi


def make_in_maps(inputs):
    x = np.asarray(inputs["x"], np.float32)
    shared = {k: np.ascontiguousarray(np.asarray(inputs[k], np.float32)[0]) for k in
              ("w_in", "w_ret_out", "w_moba_out", "w_o", "w_ffn_gate", "w_ffn_up", "w_ffn_down")}
    shared["norm1_w"] = np.asarray(inputs["norm1_w"], np.float32).reshape(1, D)
    shared["norm2_w"] = np.asarray(inputs["norm2_w"], np.float32).reshape(1, D)
    shared["q_norm_w"] = np.asarray(inputs["q_norm_w"], np.float32).reshape(64, 1)
    shared["k_norm_w"] = np.asarray(inputs["k_norm_w"], np.float32).reshape(64, 1)
    tabs = []
    for parity in range(2):
        t, xi = _const_tables(parity)
        tabs.append((t, xi))
    maps = []
    for c in range(8):
        b, parity = c // 2, c % 2
        t, xi = tabs[parity]
        m = dict(shared)
        m.update({k: v for k, v in t.items() if v is not None})
        xvv = np.zeros((SV, D), np.float32)
        if parity == 1:
            xvv[:] = x[b]
        else:
            xvv[S_HALF:] = x[b, :S_HALF]
        m["xv"] = xvv
        maps.append(m)
    return maps


_NC_CACHE = {}


def kernel(**inputs):
    if "nc" not in _NC_CACHE:
        _NC_CACHE["nc"] = build()
    nc = _NC_CACHE["nc"]
    maps = make_in_maps(inputs)
    res = run_bass_kernel_spmd(nc, maps, core_ids=list(range(8)))
    B = 4
    out = np.empty((B, 2 * S_HALF, D), np.float32)
    for c in range(8):
        out[c // 2, (c % 2) * S_HALF:(c % 2 + 1) * S_HALF] = np.asarray(res.results[c]["out"], np.float32)
    return out
```

```python
import numpy as np
import ml_dtypes
from contextlib import ExitStack
import concourse.bass as bass
import concourse.mybir as mybir
from concourse.bass_utils import run_bass_kernel_spmd

F32 = mybir.dt.float32
BF16 = mybir.dt.bfloat16
AF = mybir.ActivationFunctionType
ALU = mybir.AluOpType
AX = mybir.AxisListType

EPOCH = 30000
N_EPOCH = 4
N_DMA_SEM = 48

D = 1024
S_HALF = 4096
SV = 8192
SEG = 512
NSEG = SV // SEG
IN_COLS = 6656
FFN = 2816
NKC_F = FFN // 128
NSLOT = 3
NEGB = 30000.0


class Buf:
    __slots__ = ("name", "w", "r")

    def __init__(self, name):
        self.name = name
        self.w = None
        self.r = []


class TB:
    __slots__ = ("t", "b")

    def __init__(self, t, name):
        self.t = t
        self.b = Buf(name)


class Emitter:
    def __init__(self, nc, stack):
        self.nc = nc
        self.eng = {"pe": nc.tensor, "act": nc.scalar, "dve": nc.vector,
                    "pool": nc.gpsimd, "sp": nc.sync}
        self.cnt = {e: 0 for e in ("pe", "act", "dve", "pool")}
        self.sems = {e: [stack.enter_context(nc.semaphore(f"s_{e}{i}")) for i in range(N_EPOCH)]
                     for e in ("pe", "act", "dve", "pool")}
        self.dsem = [stack.enter_context(nc.semaphore(f"s_dma{i}")) for i in range(N_DMA_SEM)]
        self.dval = [0] * N_DMA_SEM
        self.dnext2 = [0, 0]
        self.waited = {e: {} for e in self.eng}

    def _wait(self, e, ref):
        if ref[0] == "eng":
            _, pe_, n = ref
            key = ("eng", pe_)
            if self.waited[e].get(key, 0) >= n:
                return
            self.waited[e][key] = n
            self.eng[e].wait_ge(self.sems[pe_][(n - 1) // EPOCH], (n - 1) % EPOCH + 1)
        else:
            _, si, val = ref
            key = ("dma", si)
            if self.waited[e].get(key, 0) >= val:
                return
            self.waited[e][key] = val
            self.eng[e].wait_ge(self.dsem[si], val)

    def _deps(self, e, reads, writes, is_dma=False):
        deps = []
        for b in list(reads) + list(writes):
            if b.w is not None:
                deps.append(b.w)
        for b in writes:
            deps.extend(b.r)
        for ref in deps:
            if ref[0] == "eng" and ref[1] == e and e == "pe" and not is_dma:
                continue
            self._wait(e, ref)

    def op(self, e, fn, reads=(), writes=(), inc=True):
        self._deps(e, reads, writes)
        ins = fn()
        n = self.cnt[e] + 1
        ref = ("eng", e, n)
        if inc:
            self.cnt[e] = n
            ins.then_inc(self.sems[e][(n - 1) // EPOCH], 1)
        for b in reads:
            b.r.append(ref)
            if len(b.r) > 64:
                b.r = b.r[-64:] if False else b.r
        for b in writes:
            b.w = ref
            b.r = []
        return ins

    def dma(self, q, out, in_, reads=(), writes=(), **kw):
        half = N_DMA_SEM // 2
        k = 1 if q == "pool" else 0
        si = k * half + self.dnext2[k]
        self.dnext2[k] = (self.dnext2[k] + 1) % half
        if self.dval[si] > 0:
            self._wait(q, ("dma", si, self.dval[si]))
        self._deps(q, reads, writes, is_dma=True)
        self.dval[si] += 16
        ref = ("dma", si, self.dval[si])
        self.eng[q].dma_start(out=out, in_=in_, **kw).then_inc(self.dsem[si], 16)
        for b in reads:
            b.r.append(ref)
        for b in writes:
            b.w = ref
            b.r = []
        return ref

    def barrier(self):
        for e in self.eng:
            for pe_ in self.cnt:
                if pe_ != e and self.cnt[pe_] > 0:
                    self._wait(e, ("eng", pe_, self.cnt[pe_]))
            for si in range(N_DMA_SEM):
                if self.dval[si] > 0:
                    self._wait(e, ("dma", si, self.dval[si]))


C_RQ, C_RK, C_RV, C_RG, C_MQ, C_MK, C_MV, C_GA, C_GB = 0, 512, 1024, 2048, 3072, 3584, 4096, 4608, 5632


def granule_table():
    g = []
    A = lambda src, cols, perm=False: g.append(("A", src, cols, perm))
    B = lambda src, c0, k0=0, nk=8: g.append(("B", src, c0, k0, nk))
    A("w_in", [C_RK + 128 * h for h in range(4)])
    A("w_in", [C_RK + 128 * h for h in range(4)], True)
    A("w_in", [C_MK + 128 * u for u in range(4)])
    B("w_in", C_RV); B("w_in", C_RV + 512)
    B("w_in", C_MV)
    A("w_in", [C_RQ + 128 * h for h in range(4)])
    A("w_in", [C_RQ + 128 * h for h in range(4)], True)
    A("w_in", [C_MQ + 128 * u for u in range(4)])
    A("w_in", [C_GA + 128 * u for u in range(4)]); A("w_in", [C_GA + 512 + 128 * u for u in range(4)])
    A("w_in", [C_GB + 128 * u for u in range(4)]); A("w_in", [C_GB + 512 + 128 * u for u in range(4)])
    B("w_in", C_RG); B("w_in", C_RG + 512)
    A("w_ret_out", [128 * u for u in range(4)]); A("w_ret_out", [512 + 128 * u for u in range(4)])
    g.append(("C", "w_moba_out", [128 * u for u in range(4)]))
    g.append(("C", "w_moba_out", [512 + 128 * u for u in range(4)]))
    B("w_o", 0); B("w_o", 512)
    for i in range(6):
        A("w_ffn_gate", [128 * u for u in range(4 * i, min(4 * i + 4, NKC_F))])
    for i in range(6):
        A("w_ffn_up", [128 * u for u in range(4 * i, min(4 * i + 4, NKC_F))])
    for hf in range(2):
        for k0, nk in ((0, 8), (8, 8), (16, 6)):
            B("w_ffn_down", hf * 512, k0, nk)
    return g


G_CHUNK = [float(np.exp(np.float32(128.0) * np.log1p(-np.exp2(np.float32(-5.0 - h))))) for h in range(4)]
GRAN = granule_table()
NG = len(GRAN)
G_GATE, G_UP, G_DOWN = 21, 27, 33


def build(dbg=False):
    nc = bass.Bass("TRN2", target_bir_lowering=False)
    IN = lambda name, shape, dt=F32: nc.dram_tensor(name, shape, dt, kind="ExternalInput").ap()
    SCR = lambda name, shape, dt=BF16: nc.dram_tensor(
        name, shape, dt, kind=("ExternalOutput" if dbg else "Internal")).ap()
    xv = IN("xv", [SV, D])
    wsrc = {"w_in": IN("w_in", [D, IN_COLS]), "w_ret_out": IN("w_ret_out", [D, D]),
            "w_moba_out": IN("w_moba_out", [512, D]), "w_o": IN("w_o", [D, D]),
            "w_ffn_gate": IN("w_ffn_gate", [D, FFN]), "w_ffn_up": IN("w_ffn_up", [D, FFN]),
            "w_ffn_down": IN("w_ffn_down", [FFN, D])}
    n1w = IN("norm1_w", [1, D]); n2w = IN("norm2_w", [1, D])
    qnw = IN("q_norm_w", [64, 1]); knw = IN("k_norm_w", [64, 1])
    cosk_d = IN("cosk", [128, SV]); sink_d = IN("sink", [128, SV])
    cosq_d = IN("cosq", [128, S_HALF]); sinq_d = IN("sinq", [128, S_HALF])
    gbias_d = IN("gbias", [S_HALF, 32]); ownsel_d = IN("ownsel", [S_HALF, 32])
    dmat_d = IN("dmatT", [128, 512]); xi_d = IN("xitab", [128, 512]); zeta_d = IN("zetatab", [128, 512])
    ident_d = IN("ident", [128, 128], BF16); cm_d = IN("cmask", [128, 512], BF16)
    eoh_d = IN("eonehot", [32, SV], BF16)
    out_d = nc.dram_tensor("out", [S_HALF, D], F32, kind="ExternalOutput").ap()

    Wg = nc.dram_tensor("Wg", [NG, 128, 4096], BF16, kind="Internal").ap()
    KT = SCR("KT", [8, 64, SV])
    QS = SCR("QS", [8, 96, S_HALF])
    OT = SCR("OT", [8, 64, S_HALF])
    MIXA = SCR("MIXA", [128, 8, S_HALF])
    SGB = SCR("SGB", [128, 8, S_HALF])
    B_Wg = [Buf(f"Wg{i}") for i in range(NG)]
    B_KT = [Buf(f"KT{h}") for h in range(8)]
    B_QS = [Buf(f"QS{h}") for h in range(8)]
    B_OT = Buf("OT"); B_MIXA = Buf("MIXA"); B_SGB = Buf("SGB"); B_out = Buf("out")

    with ExitStack() as st0:
        em = Emitter(nc, st0)
        op = em.op

        def SB(st, name, shape, dt):
            return TB(st.enter_context(nc.sbuf_tensor("sb_" + name, shape, dt)), name)

        def PS(st, name, shape, dt):
            return TB(st.enter_context(nc.psum_tensor("ps_" + name, shape, dt)), name)

        psF2 = [PS(st0, f"psF2_{i}", [128, 1024], F32) for i in range(3)]
        psF = [TB(psF2[i // 2].t[:, (i % 2) * 512:(i % 2 + 1) * 512], f"psF{i}") for i in range(6)]
        psB = [PS(st0, f"psB{i}", [128, 1024], BF16) for i in range(2)]
        pfi = [0]; pbi = [0]

        def pf(n=4):
            i = pfi[0] % n
            pfi[0] += 1
            return psF[i]

        def pb():
            i = pbi[0] % 2
            pbi[0] += 1
            return psB[i]

        wslot = [SB(st0, f"wslot{i}", [128, 4096], BF16) for i in range(NSLOT)]
        ident = SB(st0, "ident", [128, 128], BF16)
        onesf = SB(st0, "onesf", [128, 64], F32)
        onesb = SB(st0, "onesb", [64, 64], BF16)
        wn1 = SB(st0, "wn1", [128, D], F32)
        gq = SB(st0, "gq", [64, 1], F32)
        gk = SB(st0, "gk", [64, 1], F32)
        gk2 = SB(st0, "gk2", [128, 1], F32)
        onesblk = SB(st0, "onesblk", [128, 128], BF16)

        def cast_granule(gi):
            gd = GRAN[gi]
            W = wsrc[gd[1]]
            if gd[0] == "A":
                dst = Wg[gi].rearrange("p (kc c) -> p kc c", kc=8)
                if gd[3]:
                    for u, c0 in enumerate(gd[2]):
                        for (d0, s0) in ((0, c0 + 64), (64, c0)):
                            em.dma("pool", dst[:, :, u * 128 + d0:u * 128 + d0 + 64],
                                   W[:, s0:s0 + 64].rearrange("(kc p) c -> p kc c", p=128), writes=[B_Wg[gi]])
                else:
                    c0 = gd[2][0]
                    ncol = 128 * len(gd[2])
                    em.dma("pool", dst[:, :, 0:ncol], W[:, c0:c0 + ncol].rearrange("(kc p) c -> p kc c", p=128),
                           writes=[B_Wg[gi]])
            elif gd[0] == "B":
                _, _, c0, k0, nk = gd
                dst = Wg[gi].rearrange("p (kc c) -> p kc c", kc=8)
                em.dma("pool", dst[:, 0:nk],
                       W[k0 * 128:(k0 + nk) * 128, c0:c0 + 512].rearrange("(kc p) c -> p kc c", p=128),
                       writes=[B_Wg[gi]])
            else:
                dst = Wg[gi, 0:64].rearrange("p (h c) -> p h c", h=8)
                c0 = gd[2][0]
                em.dma("pool", dst, W[:, c0:c0 + 512].rearrange("(h p) c -> p h c", p=64), writes=[B_Wg[gi]])

        for gi in range(6):
            cast_granule(gi)
        cast_next = [6]

        def cast_more(k):
            for _ in range(k):
                if cast_next[0] < NG:
                    cast_granule(cast_next[0])
                    cast_next[0] += 1

        em.dma("sp", ident.t[:], ident_d, writes=[ident.b])
        em.dma("sp", wn1.t[:], n1w.partition_broadcast(128), writes=[wn1.b])
        em.dma("sp", gq.t[:], qnw, writes=[gq.b])
        em.dma("sp", gk.t[:], knw, writes=[gk.b])
        em.dma("sp", gk2.t[0:64, :], knw, writes=[gk2.b])
        em.dma("sp", gk2.t[64:128, :], knw, writes=[gk2.b])
        op("pool", lambda: nc.gpsimd.memset(onesblk.t[:], 0.0), writes=[onesblk.b])
        op("pool", lambda: nc.gpsimd.memset(onesblk.t[0:64, 0:64], 1.0), writes=[onesblk.b])
        op("pool", lambda: nc.gpsimd.memset(onesblk.t[64:128, 64:128], 1.0), writes=[onesblk.b])
        op("pool", lambda: nc.gpsimd.memset(onesf.t[:], 1.0), writes=[onesf.b])
        op("pool", lambda: nc.gpsimd.memset(onesb.t[:], 1.0), writes=[onesb.b])

        seq1 = []
        for s in range(NSEG):
            if s < 8:
                seq1 += [0, 1, 3, 4, 2, 5]
            else:
                seq1 += [0, 1, 6, 7, 3, 4, 13, 14, 2, 11, 5, 12, 8, 9, 15, 10, 16]
        seqC = []
        sF = [17, 18, 19, 20]
        sG = [x for i in range(6) for x in (G_GATE + i, G_UP + i)]
        sD = list(range(G_DOWN, G_DOWN + 6))
        seqC += sF
        for j in range(8):
            seqC += sG + (sF if j + 1 < 8 else []) + sD
        ws = {"seq": seq1, "slots": wslot, "pos": 0, "emit": 0}

        def wstream(seq, slots):
            ws["seq"] = seq; ws["slots"] = slots; ws["pos"] = 0; ws["emit"] = 0

        def wget(expect):
            seq, slots = ws["seq"], ws["slots"]
            i = ws["pos"]
            assert seq[i] == expect, (i, seq[i], expect)
            ns = len(slots)
            while ws["emit"] < min(len(seq), i + ns - 1):
                k = ws["emit"]
                sl = slots[k % ns]
                gi = seq[k]
                gd = GRAN[gi]
                if gd[0] == "C":
                    em.dma("sp", sl.t[0:64, :], Wg[gi, 0:64, :], reads=[B_Wg[gi]], writes=[sl.b])
                elif gd[0] == "A" and len(gd[2]) < 4:
                    nc_ = 128 * len(gd[2])
                    em.dma("sp", sl.t[:].rearrange("p (kc c) -> p kc c", kc=8)[:, :, 0:nc_],
                           Wg[gi].rearrange("p (kc c) -> p kc c", kc=8)[:, :, 0:nc_], reads=[B_Wg[gi]], writes=[sl.b])
                elif gd[0] == "B" and gd[4] < 8:
                    em.dma("sp", sl.t[:, 0:gd[4] * 512], Wg[gi, :, 0:gd[4] * 512], reads=[B_Wg[gi]], writes=[sl.b])
                else:
                    em.dma("sp", sl.t[:], Wg[gi], reads=[B_Wg[gi]], writes=[sl.b])
                ws["emit"] += 1
            ws["pos"] += 1
            return slots[i % ns]

        vB = lambda sl: sl.t[:].rearrange("p (kc c) -> p kc c", kc=8)
        vC = lambda sl: sl.t[0:64, :].rearrange("p (h c) -> p h c", h=8)

        def mm(out_ap, lhsT, rhs, start, stop, reads, wb, inc):
            op("pe", lambda: nc.tensor.matmul(out_ap, lhsT=lhsT, rhs=rhs, start=start, stop=stop),
               reads=reads, writes=[wb], inc=inc)

        def proj_fm(sl, u, hT, M=128, c0=0):
            p = pf()
            w = vB(sl)
            for kc in range(8):
                mm(p.t[0:M, :], w[:, kc, u * 128 + c0:u * 128 + c0 + M], hT.t[:, kc, :], kc == 0, kc == 7,
                   [sl.b, hT.b], p.b, kc == 7)
            return p

        def proj_tm(sl, hT, t):
            p = pf()
            w = vB(sl)
            for kc in range(8):
                mm(p.t[:, :], hT.t[:, kc, t * 128:(t + 1) * 128], w[:, kc, :], kc == 0, kc == 7,
                   [sl.b, hT.b], p.b, kc == 7)
            return p

        def rstd_small(ss, n, cols, eps):
            op("act", lambda: nc.scalar.activation(out=ss.t[:, cols], in_=ss.t[:, cols], func=AF.Ln,
                                                   bias=eps, scale=1.0 / n), reads=[ss.b], writes=[ss.b])
            op("act", lambda: nc.scalar.activation(out=ss.t[:, cols], in_=ss.t[:, cols], func=AF.Exp,
                                                   scale=-0.5), reads=[ss.b], writes=[ss.b])

        def norm_T(xt_ap, xb, wn, ss, sqj, xs, hT, t):
            op("act", lambda: nc.scalar.activation(out=sqj.t[:], in_=xt_ap, func=AF.Square,
                                                   accum_out=ss.t[:, 0:1]),
               reads=[xb], writes=[sqj.b, ss.b])
            rstd_small(ss, D, slice(0, 1), 1e-6)
            op("dve", lambda: nc.vector.scalar_tensor_tensor(out=xs.t[:], in0=xt_ap, scalar=ss.t[:, 0:1],
                                                             in1=wn.t[:], op0=ALU.mult, op1=ALU.mult),
               reads=[xb, ss.b, wn.b], writes=[xs.b])
            pt = pb()
            for kc in range(8):
                op("pe", lambda kc=kc: nc.tensor.transpose(out=pt.t[:, kc * 128:(kc + 1) * 128],
                                                           in_=xs.t[:, kc * 128:(kc + 1) * 128],
                                                           identity=ident.t[:]),
                   reads=[xs.b, ident.b], writes=[pt.b], inc=(kc == 7))
            op("act", lambda: nc.scalar.copy(out=hT.t[:, :, t * 128:(t + 1) * 128],
                                             in_=pt.t[:].rearrange("p (k c) -> p k c", k=8)),
               reads=[pt.b], writes=[hT.b])

        with ExitStack() as stV:
            V_all = SB(stV, "V_all", [128, 64, 8, 66], BF16)
            op("pool", lambda: nc.gpsimd.memset(V_all.t[:], 1.0), writes=[V_all.b])

            with ExitStack() as st1:
                xt = [SB(st1, f"xt{i}", [128, D], F32) for i in range(2)]
                xs = SB(st1, "xs", [128, D], BF16)
                ss = SB(st1, "ss", [128, 1], F32)
                hTs = [SB(st1, f"hT{i}", [128, 8, SEG], BF16) for i in range(2)]
                ck = SB(st1, "ck", [128, SEG], F32); sk = SB(st1, "sk", [128, SEG], F32)
                tmpA = SB(st1, "tmpA", [128, SEG], F32); tmpB = SB(st1, "tmpB", [128, SEG], F32)
                rkT = SB(st1, "rkT", [128, 4, SEG], BF16)
                rqT = SB(st1, "rqT", [128, 4, SEG], BF16)
                rqxT = SB(st1, "rqxT", [128, 4, SEG], BF16)
                rvS = SB(st1, "rvS", [128, 4, D], BF16)
                sg = SB(st1, "sg", [128, 4, D], BF16)
                kz = SB(st1, "kz", [128, 512], BF16)
                STt = SB(st1, "STt", [128, 512], BF16)
                ons = [SB(st1, f"on{i}", [128, 256], F32) for i in range(2)]
                ret = SB(st1, "ret", [128, D], BF16)
                retT = SB(st1, "retT", [128, 8, SEG], BF16)
                sqm = SB(st1, "sqm", [128, SEG], BF16)
                rs = SB(st1, "rs", [128, SEG], F32)
                kTs = [SB(st1, f"kTs{i}", [128, SEG], BF16) for i in range(2)]
                ksum = SB(st1, "ksum", [128, 2], F32)
                kb2 = SB(st1, "kb2", [128, 2], BF16)
                kbarT = SB(st1, "kbarT", [64, 8, 32], BF16)
                QA = SB(st1, "QA", [96, 8, SEG], BF16)
                Z = SB(st1, "Z", [128, 8, 96], BF16)
                g3 = SB(st1, "g3", [128, 8, 32], F32)
                m3 = SB(st1, "m3", [128, 8, 32], F32)
                v3 = SB(st1, "v3", [128, 8, 32], F32)
                mx = SB(st1, "mx", [128, 8, 8], F32)
                gbt = SB(st1, "gbt", [128, 32], F32); ost = SB(st1, "ost", [128, 32], F32)
                gst = SB(st1, "gst", [128, 12], F32)
                gstb = [Buf(f"gst{i}") for i in range(9)]
                sga = [SB(st1, f"sga{i}", [128, SEG], BF16) for i in range(2)]
                stg = [SB(st1, f"stg{i}", [128, SEG], BF16) for i in range(2)]
                Sst = SB(st1, "Sst", [128, 1024], F32)
                Sbf = SB(st1, "Sbf", [128, 1024], BF16)
                dmat = SB(st1, "dmat", [128, 512], F32)
                xit = SB(st1, "xit", [128, 512], F32)
                zet = SB(st1, "zet", [128, 512], F32)
                for tb, src in ((dmat, dmat_d), (xit, xi_d), (zet, zeta_d)):
                    em.dma("sp", tb.t[:], src, writes=[tb.b])
                op("pool", lambda: nc.gpsimd.memset(Sst.t[:], 0.0), writes=[Sst.b])
                op("pool", lambda: nc.gpsimd.memset(Sbf.t[:], 0.0), writes=[Sbf.b])
                op("pool", lambda: nc.gpsimd.memset(kbarT.t[:], 0.0), writes=[kbarT.b])
                op("pool", lambda: nc.gpsimd.memset(kb2.t[:], 0.0), writes=[kb2.b])
                op("pool", lambda: nc.gpsimd.memset(Z.t[:], 0.0), writes=[Z.b])
                xload = [0]

                def load_x(tile_idx):
                    tb = xt[tile_idx % 2]
                    em.dma("sp", tb.t[:], xv[tile_idx * 128:(tile_idx + 1) * 128, :], writes=[tb.b])

                load_x(0)

                def emit_norm(s_):
                    for t in range(4):
                        ti = s_ * 4 + t
                        if ti + 1 < NSEG * 4:
                            load_x(ti + 1)
                        norm_T(xt[ti % 2].t[:], xt[ti % 2].b, wn1, ss, xs, xs, hTs[s_ % 2], t)

                def rotary(p1, p2, ct, st_, dst, h):
                    op("dve", lambda: nc.vector.tensor_tensor(out=tmpA.t[:], in0=p1.t[:], in1=ct.t[:], op=ALU.mult),
                       reads=[p1.b, ct.b], writes=[tmpA.b])
                    op("dve", lambda: nc.vector.tensor_tensor(out=tmpB.t[:], in0=p2.t[:], in1=st_.t[:], op=ALU.mult),
                       reads=[p2.b, st_.b], writes=[tmpB.b])
                    op("dve", lambda: nc.vector.tensor_tensor(out=dst.t[:, h, :], in0=tmpA.t[:], in1=tmpB.t[:],
                                                              op=ALU.add),
                       reads=[tmpA.b, tmpB.b], writes=[dst.b])

                def qknorm(p, gw, dst_ap, dst_b):
                    op("act", lambda: nc.scalar.activation(out=sqm.t[0:64, :], in_=p.t[0:64, :], func=AF.Square),
                       reads=[p.b], writes=[sqm.b])
                    p2 = pf()
                    mm(p2.t[0:64, :], onesb.t[:], sqm.t[0:64, :], True, True, [onesb.b, sqm.b], p2.b, True)
                    op("act", lambda: nc.scalar.activation(out=rs.t[0:64, :], in_=p2.t[0:64, :], func=AF.Ln,
                                                           bias=1e-6, scale=1.0 / 64),
                       reads=[p2.b], writes=[rs.b])
                    op("act", lambda: nc.scalar.activation(out=rs.t[0:64, :], in_=rs.t[0:64, :], func=AF.Exp, scale=-0.5),
                       reads=[rs.b], writes=[rs.b])
                    op("dve", lambda: nc.vector.scalar_tensor_tensor(out=dst_ap, in0=p.t[0:64, :], scalar=gw.t[:, 0:1],
                                                                     in1=rs.t[0:64, :], op0=ALU.mult, op1=ALU.mult),
                       reads=[p.b, gw.b, rs.b], writes=[dst_b])

                def normA(s_, t):
                    ti = s_ * 4 + t
                    if ti + 1 < NSEG * 4:
                        load_x(ti + 1)
                    xtb = xt[ti % 2]
                    op("act", lambda: nc.scalar.activation(out=xs.t[:], in_=xtb.t[:], func=AF.Square,
                                                           accum_out=ss.t[:, 0:1]),
                       reads=[xtb.b], writes=[xs.b, ss.b])
                    rstd_small(ss, D, slice(0, 1), 1e-6)
                    op("dve", lambda: nc.vector.scalar_tensor_tensor(out=xs.t[:], in0=xtb.t[:], scalar=ss.t[:, 0:1],
                                                                     in1=wn1.t[:], op0=ALU.mult, op1=ALU.mult),
                       reads=[xtb.b, ss.b, wn1.b], writes=[xs.b])

                def normB(s_, t):
                    pt = pb()
                    for kc in range(8):
                        op("pe", lambda kc=kc: nc.tensor.transpose(out=pt.t[:, kc * 128:(kc + 1) * 128],
                                                                   in_=xs.t[:, kc * 128:(kc + 1) * 128],
                                                                   identity=ident.t[:]),
                           reads=[xs.b, ident.b], writes=[pt.b], inc=(kc == 7))
                    op("act", lambda: nc.scalar.copy(out=hTs[s_ % 2].t[:, :, t * 128:(t + 1) * 128],
                                                     in_=pt.t[:].rearrange("p (k c) -> p k c", k=8)),
                       reads=[pt.b], writes=[hTs[s_ % 2].b])

                emit_norm(0)
                for s in range(NSEG):
                    own = s >= 8
                    j = s - 8
                    cs = slice(s * SEG, (s + 1) * SEG)
                    cast_more(3)
                    em.dma("sp", ck.t[:], cosk_d[:, cs], writes=[ck.b])
                    em.dma("sp", sk.t[:], sink_d[:, cs], writes=[sk.b])
                    hT = hTs[s % 2]
                    s0 = wget(0); s1 = wget(1)
                    for h in range(4):
                        rotary(proj_fm(s0, h, hT), proj_fm(s1, h, hT), ck, sk, rkT, h)
                    if own:
                        em.dma("sp", ck.t[:], cosq_d[:, j * SEG:(j + 1) * SEG], writes=[ck.b])
                        em.dma("sp", sk.t[:], sinq_d[:, j * SEG:(j + 1) * SEG], writes=[sk.b])
                        s0 = wget(6); s1 = wget(7)
                        for h in range(4):
                            rotary(proj_fm(s0, h, hT), proj_fm(s1, h, hT), ck, sk, rqT, h)
                            op("pool", lambda h=h: nc.gpsimd.tensor_tensor(
                                out=rqxT.t[:, h, :].rearrange("p (c n) -> p c n", c=4),
                                in0=rqT.t[:, h, :].rearrange("p (c n) -> p c n", c=4),
                                in1=xit.t[:, None, h * 128:(h + 1) * 128].to_broadcast([128, 4, 128]), op=ALU.mult),
                               reads=[rqT.b, xit.b], writes=[rqxT.b])
                    sv = [wget(3), wget(4)]
                    for t in range(4):
                        for hf in range(2):
                            p = proj_tm(sv[hf], hT, t)
                            op("act", lambda p=p, t=t, hf=hf: nc.scalar.copy(
                                out=rvS.t[:, t, hf * 512:(hf + 1) * 512], in_=p.t[:]),
                               reads=[p.b], writes=[rvS.b])
                    if own:
                        sgw = [wget(13), wget(14)]
                        for t in range(4):
                            for hf in range(2):
                                p = proj_tm(sgw[hf], hT, t)
                                op("act", lambda p=p, t=t, hf=hf: nc.scalar.activation(
                                    out=sg.t[:, t, hf * 512:(hf + 1) * 512], in_=p.t[:], func=AF.Silu),
                                   reads=[p.b], writes=[sg.b])
                    pos_all = {}

                    def chunk_s1(c):
                        cc = slice(c * 128, (c + 1) * 128)
                        pos_ = []
                        pos_all[c] = pos_
                        if own:
                            psc = pf()
                            for h in range(4):
                                mm(psc.t[:, h * 128:(h + 1) * 128], rkT.t[:, h, cc], rqT.t[:, h, cc], True, True,
                                   [rkT.b, rqT.b], psc.b, h == 3)
                            op("dve", lambda psc=psc: nc.vector.tensor_tensor(out=STt.t[:], in0=psc.t[:], in1=dmat.t[:],
                                                                             op=ALU.mult),
                               reads=[psc.b, dmat.b], writes=[STt.b])
                            for hp in range(2):
                                po = psF[4 + hp]
                                pos_.append(po)
                                for hh in range(2):
                                    h = hp * 2 + hh
                                    mm(po.t[:, hh * 256:(hh + 1) * 256], STt.t[:, h * 128:(h + 1) * 128],
                                       rvS.t[:, c, h * 256:(h + 1) * 256], True, False, [STt.b, rvS.b], po.b, False)
                                    mm(po.t[:, hh * 256:(hh + 1) * 256], rqxT.t[:, h, cc],
                                       Sbf.t[:, h * 256:(h + 1) * 256], False, True, [rqxT.b, Sbf.b], po.b, hh == 1)
                        pt = pb()
                        for h in range(4):
                            op("pe", lambda h=h, pt=pt, cc=cc: nc.tensor.transpose(
                                out=pt.t[:, h * 128:(h + 1) * 128], in_=rkT.t[:, h, cc], identity=ident.t[:]),
                               reads=[rkT.b, ident.b], writes=[pt.b], inc=(h == 3))
                        op("dve", lambda pt=pt: nc.vector.tensor_tensor(out=kz.t[:], in0=pt.t[:, 0:512], in1=zet.t[:],
                                                                       op=ALU.mult),
                           reads=[pt.b, zet.b], writes=[kz.b])
                        for hp in range(2):
                            pu = pf()
                            for hh in range(2):
                                h = hp * 2 + hh
                                mm(pu.t[:, hh * 256:(hh + 1) * 256], kz.t[:, h * 128:(h + 1) * 128],
                                   rvS.t[:, c, h * 256:(h + 1) * 256], True, True, [kz.b, rvS.b], pu.b, hh == 1)
                            for hh in range(2):
                                h = hp * 2 + hh
                                hs = slice(h * 256, (h + 1) * 256)
                                op("dve", lambda hs=hs, pu=pu, hh=hh, h=h: nc.vector.scalar_tensor_tensor(
                                    out=Sst.t[:, hs], in0=Sst.t[:, hs], scalar=float(G_CHUNK[h]),
                                    in1=pu.t[:, hh * 256:(hh + 1) * 256], op0=ALU.mult, op1=ALU.add),
                                   reads=[Sst.b, pu.b], writes=[Sst.b])
                        op("act", lambda: nc.scalar.copy(out=Sbf.t[:], in_=Sst.t[:]), reads=[Sst.b], writes=[Sbf.b])

                    def chunk_s2(c):
                        cc = slice(c * 128, (c + 1) * 128)
                        pos_ = pos_all[c]
                        if own:
                            for h in range(4):
                                oap = pos_[h // 2].t[:, (h % 2) * 256:(h % 2 + 1) * 256]
                                op("act", lambda oap=oap, h=h: nc.scalar.activation(
                                    out=ret.t[:, h * 256:(h + 1) * 256], in_=oap, func=AF.Identity,
                                    accum_out=gst.t[:, h:h + 1]), reads=[pos_[h // 2].b], writes=[gstb[h], ret.b])
                                op("act", lambda oap=oap, h=h: nc.scalar.activation(
                                    out=ret.t[:, h * 256:(h + 1) * 256], in_=oap, func=AF.Square,
                                    accum_out=gst.t[:, 4 + h:5 + h]), reads=[pos_[h // 2].b], writes=[gstb[4 + h], ret.b])
                            op("dve", lambda: nc.vector.tensor_scalar(
                                out=gst.t[:, 0:4], in0=gst.t[:, 0:4], scalar1=1.0 / 256, scalar2=None, op0=ALU.mult),
                               reads=gstb[0:4], writes=gstb[0:4])
                            op("dve", lambda: nc.vector.tensor_tensor(
                                out=gst.t[:, 8:12], in0=gst.t[:, 0:4], in1=gst.t[:, 0:4], op=ALU.mult),
                               reads=gstb[0:4], writes=[gstb[8]])
                            op("dve", lambda: nc.vector.scalar_tensor_tensor(
                                out=gst.t[:, 4:8], in0=gst.t[:, 4:8], scalar=1.0 / 256, in1=gst.t[:, 8:12],
                                op0=ALU.mult, op1=ALU.subtract), reads=gstb[4:9], writes=gstb[4:8])
                            op("act", lambda: nc.scalar.activation(out=gst.t[:, 4:8], in_=gst.t[:, 4:8], func=AF.Ln,
                                                                   bias=1e-5, scale=1.0),
                               reads=gstb[4:8], writes=gstb[4:8])
                            op("act", lambda: nc.scalar.activation(out=gst.t[:, 4:8], in_=gst.t[:, 4:8], func=AF.Exp,
                                                                   scale=-0.5),
                               reads=gstb[4:8], writes=gstb[4:8])
                            op("dve", lambda: nc.vector.scalar_tensor_tensor(
                                out=gst.t[:, 8:12], in0=gst.t[:, 0:4], scalar=-1.0, in1=gst.t[:, 4:8],
                                op0=ALU.mult, op1=ALU.mult), reads=gstb[0:8], writes=[gstb[8]])
                            for h in range(4):
                                oap = pos_[h // 2].t[:, (h % 2) * 256:(h % 2 + 1) * 256]
                                onb = ons[h % 2]
                                op("act", lambda oap=oap, h=h, onb=onb: nc.scalar.activation(
                                    out=onb.t[:], in_=oap, func=AF.Identity, scale=gst.t[:, 4 + h:5 + h],
                                    bias=gst.t[:, 8 + h:9 + h]), reads=[pos_[h // 2].b] + gstb[0:9], writes=[onb.b])
                                op("dve", lambda h=h, c=c, onb=onb: nc.vector.tensor_tensor(
                                    out=ret.t[:, h * 256:(h + 1) * 256], in0=onb.t[:],
                                    in1=sg.t[:, c, h * 256:(h + 1) * 256], op=ALU.mult),
                                   reads=[onb.b, sg.b], writes=[ret.b])
                            pt = pb()
                            for kc in range(8):
                                op("pe", lambda kc=kc, pt=pt: nc.tensor.transpose(
                                    out=pt.t[:, kc * 128:(kc + 1) * 128], in_=ret.t[:, kc * 128:(kc + 1) * 128],
                                    identity=ident.t[:]), reads=[ret.b, ident.b], writes=[pt.b], inc=(kc == 7))
                            op("act", lambda pt=pt, cc=cc: nc.scalar.copy(
                                out=retT.t[:, :, cc], in_=pt.t[:].rearrange("p (k c) -> p k c", k=8)),
                               reads=[pt.b], writes=[retT.b])

                    def mk_pair(u, s2_):
                        p = proj_fm(s2_, u, hT)
                        kst = kTs[u % 2]
                        op("act", lambda: nc.scalar.activation(out=sqm.t[:], in_=p.t[:], func=AF.Square),
                           reads=[p.b], writes=[sqm.b])
                        p2 = pf()
                        mm(p2.t[:], onesblk.t[:], sqm.t[:], True, True, [onesblk.b, sqm.b], p2.b, True)
                        op("act", lambda: nc.scalar.activation(out=rs.t[:], in_=p2.t[:], func=AF.Ln, bias=1e-6,
                                                               scale=1.0 / 64), reads=[p2.b], writes=[rs.b])
                        op("act", lambda: nc.scalar.activation(out=rs.t[:], in_=rs.t[:], func=AF.Exp, scale=-0.5),
                           reads=[rs.b], writes=[rs.b])
                        op("dve", lambda: nc.vector.scalar_tensor_tensor(out=kst.t[:], in0=p.t[:], scalar=gk2.t[:, 0:1],
                                                                         in1=rs.t[:], op0=ALU.mult, op1=ALU.mult),
                           reads=[p.b, gk2.b, rs.b], writes=[kst.b])
                        op("dve", lambda: nc.vector.reduce_sum(
                            out=ksum.t[:], in_=kst.t[:].rearrange("p (b k) -> p b k", b=2), axis=AX.X),
                           reads=[kst.b], writes=[ksum.b])
                        op("dve", lambda: nc.vector.tensor_scalar(
                            out=kbarT.t[:, 2 * u, 2 * s:2 * s + 2], in0=ksum.t[0:64, :], scalar1=1.0 / 256, scalar2=None,
                            op0=ALU.mult), reads=[ksum.b], writes=[kbarT.b])
                        op("dve", lambda: nc.vector.tensor_scalar(
                            out=kb2.t[64:128, :], in0=ksum.t[64:128, :], scalar1=1.0 / 256, scalar2=None,
                            op0=ALU.mult), reads=[ksum.b], writes=[kb2.b])
                        pk = pf()
                        mm(pk.t[0:64, 0:2], ident.t[:, 64:128], kb2.t[:], True, True, [ident.b, kb2.b], pk.b, True)
                        op("act", lambda: nc.scalar.copy(out=kbarT.t[:, 2 * u + 1, 2 * s:2 * s + 2], in_=pk.t[0:64, 0:2]),
                           reads=[pk.b], writes=[kbarT.b])
                        em.dma("pool", KT[2 * u:2 * u + 2, :, cs].rearrange("h p c -> (h p) c"), kst.t[:],
                               reads=[kst.b], writes=[B_KT[2 * u], B_KT[2 * u + 1]])

                    def mv_step(s5_):
                        for t in range(4):
                            p = proj_tm(s5_, hT, t)
                            op("act", lambda p=p, t=t, s=s: nc.scalar.copy(
                                out=V_all.t[:, s * 4 + t, :, 0:64], in_=p.t[:].rearrange("p (h d) -> p h d", h=8)),
                               reads=[p.b], writes=[V_all.b])

                    def mq_step(h, s8_):
                        u, e = h // 2, h % 2
                        p = proj_fm(s8_, u, hT, M=64, c0=64 * e)
                        qknorm(p, gq, QA.t[0:64, h, :], QA.b)

                    def gate_s1(t):
                        qs = slice(j * SEG + t * 128, j * SEG + (t + 1) * 128)
                        em.dma("sp", gbt.t[:], gbias_d[qs, :], writes=[gbt.b])
                        em.dma("sp", ost.t[:], ownsel_d[qs, :], writes=[ost.b])
                        pg = pf()
                        for h in range(8):
                            mm(pg.t[:, h * 32:(h + 1) * 32], QA.t[0:64, h, t * 128:(t + 1) * 128], kbarT.t[:, h, :],
                               True, True, [QA.b, kbarT.b], pg.b, h == 7)
                        op("dve", lambda pg=pg: nc.vector.tensor_tensor(
                            out=g3.t[:], in0=pg.t[:, 0:256].rearrange("p (h n) -> p h n", h=8),
                            in1=gbt.t[:, None, :].to_broadcast([128, 8, 32]), op=ALU.add),
                           reads=[pg.b, gbt.b], writes=[g3.b])
                        for h in range(8):
                            op("dve", lambda h=h: nc.vector.max(out=mx.t[:, h, :], in_=g3.t[:, h, :]),
                               reads=[g3.b], writes=[mx.b])
                        op("dve", lambda: nc.vector.tensor_tensor(
                            out=m3.t[:], in0=g3.t[:], in1=mx.t[:, :, 2:3].to_broadcast([128, 8, 32]), op=ALU.is_ge),
                           reads=[g3.b, mx.b], writes=[m3.b])
                        op("dve", lambda: nc.vector.tensor_single_scalar(out=v3.t[:], in_=g3.t[:], scalar=-1e29,
                                                                         op=ALU.is_gt),
                           reads=[g3.b], writes=[v3.b])
                        op("dve", lambda: nc.vector.tensor_tensor(out=m3.t[:], in0=m3.t[:], in1=v3.t[:], op=ALU.mult),
                           reads=[m3.b, v3.b], writes=[m3.b])
                        op("dve", lambda: nc.vector.tensor_tensor(
                            out=m3.t[:], in0=m3.t[:], in1=ost.t[:, None, :].to_broadcast([128, 8, 32]), op=ALU.add),
                           reads=[m3.b, ost.b], writes=[m3.b])
                        op("dve", lambda: nc.vector.tensor_scalar(
                            out=Z.t[:, :, 64:96], in0=m3.t[:], scalar1=-1.0, scalar2=NEGB, op0=ALU.add, op1=ALU.mult),
                           reads=[m3.b], writes=[Z.b])

                    def gate_s2(t):
                        pt = pb()
                        for h in range(8):
                            op("pe", lambda h=h, pt=pt: nc.tensor.transpose(
                                out=pt.t[0:96, h * 128:(h + 1) * 128], in_=Z.t[:, h, :], identity=ident.t[:]),
                               reads=[Z.b, ident.b], writes=[pt.b], inc=(h == 7))
                        op("act", lambda pt=pt, t=t: nc.scalar.copy(
                            out=QA.t[64:96, :, t * 128:(t + 1) * 128],
                            in_=pt.t[64:96, :].rearrange("p (h c) -> p h c", h=8)),
                           reads=[pt.b], writes=[QA.b])

                    def fin_a(sa, sr, half, u):
                        uu = half * 4 + u
                        pga = proj_fm(sa, u, hT)
                        sgt = sga[uu % 2]
                        op("act", lambda pga=pga, sgt=sgt: nc.scalar.activation(out=sgt.t[:], in_=pga.t[:],
                                                                                func=AF.Sigmoid),
                           reads=[pga.b], writes=[sgt.b])
                        pa = proj_fm(sr, u, retT)
                        sg_ = stg[uu % 2]
                        op("dve", lambda pa=pa, sgt=sgt, sg_=sg_: nc.vector.tensor_tensor(
                            out=sg_.t[:], in0=pa.t[:], in1=sgt.t[:], op=ALU.mult),
                           reads=[pa.b, sgt.b], writes=[sg_.b])
                        em.dma("pool", MIXA[:, uu, j * SEG:(j + 1) * SEG], sg_.t[:], reads=[sg_.b], writes=[B_MIXA])

                    def fin_b(sb_, half, u):
                        uu = half * 4 + u
                        pgb = proj_fm(sb_, u, hT)
                        sgt = sga[uu % 2]
                        op("act", lambda pgb=pgb, sgt=sgt: nc.scalar.activation(out=sgt.t[:], in_=pgb.t[:],
                                                                                func=AF.Sigmoid),
                           reads=[pgb.b], writes=[sgt.b])
                        em.dma("pool", SGB[:, uu, j * SEG:(j + 1) * SEG], sgt.t[:], reads=[sgt.b], writes=[B_SGB])

                    s2_ = wget(2)
                    N_ = s + 1 < NSEG

                    def nA(t):
                        if N_:
                            normA(s + 1, t)

                    def nB(t):
                        if N_:
                            normB(s + 1, t)

                    nA(0); chunk_s1(0); mk_pair(0, s2_); nB(0); nA(1)
                    if own:
                        chunk_s2(0)
                    mk_pair(1, s2_); nB(1); nA(2)
                    chunk_s1(1); mk_pair(2, s2_); nB(2); nA(3)
                    if own:
                        chunk_s2(1)
                    mk_pair(3, s2_); nB(3)
                    if own:
                        sb_ = wget(11)
                        for u in range(4):
                            fin_b(sb_, 0, u)
                    s5_ = wget(5)
                    chunk_s1(2); mv_step(s5_)
                    if own:
                        chunk_s2(2)
                        sb_ = wget(12)
                        for u in range(4):
                            fin_b(sb_, 1, u)
                    if not own:
                        chunk_s1(3)
                        continue
                    s8_ = wget(8)
                    mq_step(0, s8_); mq_step(1, s8_)
                    chunk_s1(3); mq_step(2, s8_); mq_step(3, s8_)
                    chunk_s2(3)
                    for h in range(4, 8):
                        mq_step(h, s8_)
                    sa = wget(9); sr = wget(15)
                    gate_s1(0); fin_a(sa, sr, 0, 0); fin_a(sa, sr, 0, 1)
                    gate_s2(0); gate_s1(1); fin_a(sa, sr, 0, 2); fin_a(sa, sr, 0, 3)
                    sa = wget(10); sr = wget(16)
                    gate_s2(1); gate_s1(2); fin_a(sa, sr, 1, 0); fin_a(sa, sr, 1, 1)
                    gate_s2(2); gate_s1(3); fin_a(sa, sr, 1, 2); fin_a(sa, sr, 1, 3)
                    gate_s2(3)
                    for h in range(8):
                        em.dma("pool", QS[h, :, j * SEG:(j + 1) * SEG], QA.t[:, h, :], reads=[QA.b], writes=[B_QS[h]])
                cast_more(NG)
                em.barrier()

            with ExitStack() as stB:
                KA = [SB(stB, f"KA{i}", [96, SV], BF16) for i in range(2)]
                QB = [SB(stB, f"QB{i}", [96, S_HALF], BF16) for i in range(2)]
                ptile = [SB(stB, f"ptile{i}", [128, 2 * SEG], BF16) for i in range(3)]
                cm = SB(stB, "cm", [128, 512], BF16)
                rl = SB(stB, "rl", [128, SEG], F32)
                osb = SB(stB, "osb", [64, SEG], F32)
                ot = [SB(stB, f"ot{i}", [64, SEG], BF16) for i in range(2)]
                em.dma("sp", cm.t[:], cm_d, writes=[cm.b])
                for i in range(2):
                    em.dma("sp", KA[i].t[64:96, :], eoh_d, writes=[KA[i].b])
                psO = [psF[4], psF[5]]
                it = 0
                sp_cnt = [0]
                pend_epi = []
                pbc = TB(psB[0].t[:].bitcast(F32), "pbc")
                def loadKQ(h_):
                    em.dma("sp", KA[h_ % 2].t[0:64, :], KT[h_], reads=[B_KT[h_]], writes=[KA[h_ % 2].b])
                    em.dma("sp", QB[h_ % 2].t[:], QS[h_], reads=[B_QS[h_]], writes=[QB[h_ % 2].b])

                loadKQ(0)
                for h in range(8):
                    ka = KA[h % 2]; qb = QB[h % 2]
                    if h + 1 < 8:
                        loadKQ(h + 1)
                    for j in range(8):
                        nkt = 4 * (8 + j + 1)
                        po = psO[it % 2]
                        qsl = slice(j * SEG, (j + 1) * SEG)
                        pi = 0

                        def s_mm(kt):
                            ps_ = pf()
                            mm(ps_.t[:], ka.t[:, kt * 128:(kt + 1) * 128], qb.t[:, qsl], True, True,
                               [ka.b, qb.b], ps_.b, True)
                            return ps_

                        ps_q = [s_mm(k_) for k_ in range(min(2, nkt))]
                        while pend_epi:
                            pend_epi.pop(0)()
                        for kt in range(nkt):
                            ps_cur = ps_q.pop(0)
                            if kt + 2 < nkt:
                                ps_q.append(s_mm(kt + 2))
                            ptb = ptile[pi % 3]; pi += 1
                            op("act", lambda ps_cur=ps_cur, ptb=ptb: nc.scalar.activation(
                                out=ptb.t[:, 0:SEG], in_=ps_cur.t[:], func=AF.Exp, scale=0.125),
                               reads=[ps_cur.b], writes=[ptb.b])
                            dk = kt - (nkt - 4)
                            if dk >= 0:
                                q0 = 0 if dk < 2 else 256
                                mcol = (dk % 2) * 256
                                op("dve", lambda ptb=ptb, q0=q0, mcol=mcol: nc.vector.tensor_tensor(
                                    out=ptb.t[:, q0:q0 + 256], in0=ptb.t[:, q0:q0 + 256],
                                    in1=cm.t[:, mcol:mcol + 256], op=ALU.mult),
                                   reads=[ptb.b, cm.b], writes=[ptb.b])
                            mm(po.t[0:66, :], V_all.t[:, kt, h, :], ptb.t[:, 0:SEG], kt == 0, kt == nkt - 1,
                               [V_all.b, ptb.b], po.b, kt == nkt - 1)
                        def epilogue(po=po, h=h, qsl=qsl, otb=ot[it % 2]):
                            op("act", lambda: nc.scalar.activation(out=rl.t[64:65, :], in_=po.t[64:65, :], func=AF.Ln),
                               reads=[po.b], writes=[rl.b])
                            op("act", lambda: nc.scalar.activation(out=rl.t[64:65, :], in_=rl.t[64:65, :], func=AF.Exp,
                                                                   scale=-1.0), reads=[rl.b], writes=[rl.b])
                            mm(pbc.t[0:64, :], onesf.t[64:65, 0:64], rl.t[64:65, :], True, True, [onesf.b, rl.b], pbc.b, True)
                            op("dve", lambda: nc.vector.tensor_copy(out=osb.t[:], in_=pbc.t[0:64, :]),
                               reads=[pbc.b], writes=[osb.b])
                            op("dve", lambda: nc.vector.tensor_tensor(
                                out=otb.t[:], in0=po.t[0:64, :], in1=osb.t[:], op=ALU.mult),
                               reads=[osb.b, po.b], writes=[otb.b])
                            em.dma("pool", OT[h, :, qsl], otb.t[:], reads=[otb.b], writes=[B_OT])

                        pend_epi.append(epilogue)
                        it += 1
                while pend_epi:
                    pend_epi.pop(0)()
                em.barrier()

        with ExitStack() as stC:
            wstream(seqC, wslot + [SB(stC, f"wslotC{i}", [128, 4096], BF16) for i in range(5)])
            wn2 = SB(stC, "wn2", [128, D], F32)
            em.dma("sp", wn2.t[:], n2w.partition_broadcast(128), writes=[wn2.b])
            x1s = [SB(stC, f"x1_{i}", [128, 4, D], F32) for i in range(2)]
            otls = [SB(stC, f"otl{i}", [64, 8, SEG], BF16) for i in range(2)]
            mxas = [SB(stC, f"mxa{i}", [128, 8, SEG], BF16) for i in range(2)]
            sgbls = [SB(stC, f"sgbl{i}", [128, 8, SEG], BF16) for i in range(2)]
            mixT = SB(stC, "mixT", [128, 8, SEG], BF16)
            tmpc = SB(stC, "tmpc", [128, SEG], F32)
            h2T = SB(stC, "h2T", [128, 8, SEG], BF16)
            actT = SB(stC, "actT", [128, NKC_F, SEG], BF16)
            sil = [SB(stC, f"sil{i}", [128, SEG], BF16) for i in range(2)]
            sqj = SB(stC, "sqjc", [128, D], BF16)
            xs = SB(stC, "xsc", [128, D], BF16)
            ss = SB(stC, "ssc", [128, 1], F32)
            yo = [SB(stC, f"yo{i}", [128, 512], F32) for i in range(2)]
            yic = [0]

            def loadC(j):
                qsl = slice(j * SEG, (j + 1) * SEG)
                x1 = x1s[j % 2]; otl = otls[j % 2]; mxa = mxas[j % 2]; sgbl = sgbls[j % 2]
                em.dma("sp", x1.t[:], xv[S_HALF + j * SEG:S_HALF + (j + 1) * SEG, :].rearrange("(t p) d -> p t d", p=128),
                       writes=[x1.b])
                em.dma("pool", otl.t[:], OT[:, :, qsl].rearrange("h p c -> p h c"), reads=[B_OT], writes=[otl.b])
                em.dma("pool", mxa.t[:], MIXA[:, :, qsl], reads=[B_MIXA], writes=[mxa.b])
                em.dma("pool", sgbl.t[:], SGB[:, :, qsl], reads=[B_SGB], writes=[sgbl.b])

            loadC(0)

            def front(j):
                x1 = x1s[j % 2]; otl = otls[j % 2]; mxa = mxas[j % 2]; sgbl = sgbls[j % 2]
                for half in range(2):
                    sm = wget(17 + half)
                    w = vC(sm)
                    for u in range(4):
                        uu = half * 4 + u
                        p = pf()
                        for h in range(8):
                            mm(p.t[:], w[:, h, u * 128:(u + 1) * 128], otl.t[:, h, :], h == 0, h == 7, [sm.b, otl.b], p.b, h == 7)
                        op("dve", lambda p=p, uu=uu: nc.vector.tensor_tensor(out=tmpc.t[:], in0=p.t[:],
                                                                            in1=sgbl.t[:, uu, :], op=ALU.mult),
                           reads=[p.b, sgbl.b], writes=[tmpc.b])
                        op("pool", lambda uu=uu: nc.gpsimd.tensor_tensor(out=mixT.t[:, uu, :], in0=tmpc.t[:],
                                                                         in1=mxa.t[:, uu, :], op=ALU.add),
                           reads=[tmpc.b, mxa.b], writes=[mixT.b])
                so = [wget(19), wget(20)]
                for t in range(4):
                    for hf in range(2):
                        p = pf()
                        w = vB(so[hf])
                        for kc in range(8):
                            mm(p.t[:], mixT.t[:, kc, t * 128:(t + 1) * 128], w[:, kc, :], kc == 0, kc == 7,
                               [so[hf].b, mixT.b], p.b, kc == 7)
                        op("dve", lambda p=p, t=t, hf=hf: nc.vector.tensor_tensor(
                            out=x1.t[:, t, hf * 512:(hf + 1) * 512], in0=x1.t[:, t, hf * 512:(hf + 1) * 512],
                            in1=p.t[:], op=ALU.add), reads=[x1.b, p.b], writes=[x1.b])
                for t in range(4):
                    norm_T(x1.t[:, t, :], x1.b, wn2, ss, sqj, xs, h2T, t)

            def gateup(j):
                for i in range(6):
                    sgt_ = wget(G_GATE + i)
                    sup = wget(G_UP + i)
                    for u in range(min(4, NKC_F - 4 * i)):
                        uu = 4 * i + u
                        pg_ = proj_fm(sgt_, u, h2T)
                        sl_ = sil[uu % 2]
                        op("act", lambda pg_=pg_, sl_=sl_: nc.scalar.activation(out=sl_.t[:], in_=pg_.t[:], func=AF.Silu),
                           reads=[pg_.b], writes=[sl_.b])
                        pu_ = proj_fm(sup, u, h2T)
                        op("dve", lambda pu_=pu_, sl_=sl_, uu=uu: nc.vector.tensor_tensor(
                            out=actT.t[:, uu, :], in0=pu_.t[:], in1=sl_.t[:], op=ALU.mult),
                           reads=[pu_.b, sl_.b], writes=[actT.b])

            def down(j):
                x1 = x1s[j % 2]; otl = otls[j % 2]; mxa = mxas[j % 2]; sgbl = sgbls[j % 2]
                for hf in range(2):
                    for k in range(3):
                        sl_ = wget(G_DOWN + hf * 3 + k)
                        nk = 8 if k < 2 else NKC_F - 16
                        for t in range(4):
                            p = psF[t]
                            for kk in range(nk):
                                kc = k * 8 + kk
                                mm(p.t[:], actT.t[:, kc, t * 128:(t + 1) * 128], vB(sl_)[:, kk, :], kc == 0,
                                   kc == NKC_F - 1, [sl_.b, actT.b], p.b, kk == nk - 1)
                    for t in range(4):
                        p = psF[t]
                        y = yo[yic[0] % 2]; yic[0] += 1
                        op("dve", lambda p=p, t=t, hf=hf, y=y: nc.vector.tensor_tensor(
                            out=y.t[:], in0=x1.t[:, t, hf * 512:(hf + 1) * 512], in1=p.t[:], op=ALU.add),
                           reads=[x1.b, p.b], writes=[y.b])
                        em.dma("pool", out_d[j * SEG + t * 128:j * SEG + (t + 1) * 128, hf * 512:(hf + 1) * 512],
                               y.t[:], reads=[y.b], writes=[B_out])

            front(0)
            for j in range(8):
                if j + 1 < 8:
                    loadC(j + 1)
                gateup(j)
                if j + 1 < 8:
                    front(j + 1)
                down(j)
            em.barrier()
    return nc


def _const_tables(parity):
    bf = ml_dtypes.bfloat16
    half = 64
    inv = (np.float32(10000.0) ** (-np.arange(half, dtype=np.float32) / np.float32(half))).astype(np.float32)
    pos = np.arange(SV, dtype=np.float32)
    if parity == 0:
        pos = np.where(pos >= S_HALF, pos - S_HALF, 0.0).astype(np.float32)
    ang = (pos[:, None] * inv[None, :]).astype(np.float32)
    cos = np.cos(ang).astype(np.float32).T
    sin = np.sin(ang).astype(np.float32).T
    cosT = np.concatenate([cos, cos], 0)
    sinT = np.concatenate([-sin, sin], 0)
    sc = np.float32(128.0 ** -0.5)
    t = {"cosk": cosT * sc, "sink": sinT * sc,
         "cosq": np.ascontiguousarray(cosT[:, S_HALF:]), "sinq": np.ascontiguousarray(sinT[:, S_HALF:])}
    q = np.arange(S_HALF)
    vb = 16 + q // 256
    n = np.arange(32)
    valid = n[None, :] < vb[:, None]
    if parity == 0:
        valid &= n[None, :] >= 16
    t["gbias"] = np.where(valid, 0.0, -1e30).astype(np.float32)
    t["ownsel"] = (n[None, :] == vb[:, None]).astype(np.float32)
    lg = np.log1p(-np.exp2(-5.0 - np.arange(4, dtype=np.float32))).astype(np.float32)
    idx = np.arange(128, dtype=np.float32)
    diff = idx[None, :] - idx[:, None]
    dmT = np.where(diff >= 0, np.exp(np.maximum(diff, 0)[None] * lg[:, None, None]), 0.0).astype(np.float32)
    t["dmatT"] = np.ascontiguousarray(dmT.transpose(1, 0, 2).reshape(128, 512))
    xi = np.exp((idx + 1.0)[None, :] * lg[:, None]).astype(np.float32)
    t["xitab"] = np.ascontiguousarray(np.broadcast_to(xi.reshape(1, 512), (128, 512))).astype(np.float32)
    zeta = np.exp((127.0 - idx)[None, :] * lg[:, None]).astype(np.float32)
    t["zetatab"] = np.ascontiguousarray(np.repeat(zeta.T[:, :, None], 128, 2).reshape(128, 512))
    t["ident"] = np.eye(128, dtype=np.float32).astype(bf)
    m = np.arange(128)[:, None]; nn = np.arange(256)[None, :]
    t["cmask"] = np.concatenate([(m <= nn), (m + 128 <= nn)], 1).astype(np.float32).astype(bf)
    t["eonehot"] = (np.arange(SV)[None, :] // 256 == np.arange(32)[:, None]).astype(np.float32).astype(bf)
    return t, xi


def make_in_maps(inputs):
    x = np.asarray(inputs["x"], np.float32)
    shared = {k: np.ascontiguousarray(np.asarray(inputs[k], np.float32)[0]) for k in
              ("w_in", "w_ret_out", "w_moba_out", "w_o", "w_ffn_gate", "w_ffn_up", "w_ffn_down")}
    shared["norm1_w"] = np.asarray(inputs["norm1_w"], np.float32).reshape(1, D)
    shared["norm2_w"] = np.asarray(inputs["norm2_w"], np.float32).reshape(1, D)
    shared["q_norm_w"] = np.asarray(inputs["q_norm_w"], np.float32).reshape(64, 1)
    shared["k_norm_w"] = np.asarray(inputs["k_norm_w"], np.float32).reshape(64, 1)
    tabs = []
    for parity in range(2):
        t, xi = _const_tables(parity)
        tabs.append((t, xi))
    maps = []
    for c in range(8):
        b, parity = c // 2, c % 2
        t, xi = tabs[parity]
        m = dict(shared)
        m.update({k: v for k, v in t.items() if v is not None})
        xvv = np.zeros((SV, D), np.float32)
        if parity == 1:
            xvv[:] = x[b]
        else:
            xvv[S_HALF:] = x[b, :S_HALF]
        m["xv"] = xvv
        maps.append(m)
    return maps


_NC_CACHE = {}


def kernel(**inputs):
    if "nc" not in _NC_CACHE:
        _NC_CACHE["nc"] = build()
    nc = _NC_CACHE["nc"]
    maps = make_in_maps(inputs)
    res = run_bass_kernel_spmd(nc, maps, core_ids=list(range(8)))
    B = 4
    out = np.empty((B, 2 * S_HALF, D), np.float32)
    for c in range(8):
        out[c // 2, (c % 2) * S_HALF:(c % 2 + 1) * S_HALF] = np.asarray(res.results[c]["out"], np.float32)
    return out
```

```python
import numpy as np
import ml_dtypes
from contextlib import ExitStack
import concourse.bass as bass
import concourse.mybir as mybir
from concourse.bass_utils import run_bass_kernel_spmd

F32 = mybir.dt.float32
BF16 = mybir.dt.bfloat16
AF = mybir.ActivationFunctionType
ALU = mybir.AluOpType
AX = mybir.AxisListType

EPOCH = 30000
N_EPOCH = 4
N_DMA_SEM = 48

D = 1024
S_HALF = 4096
SV = 8192
SEG = 512
NSEG = SV // SEG
IN_COLS = 6656
FFN = 2816
NKC_F = FFN // 128
NSLOT = 3
NEGB = 30000.0


class Buf:
    __slots__ = ("name", "w", "r")

    def __init__(self, name):
        self.name = name
        self.w = None
        self.r = []


class TB:
    __slots__ = ("t", "b")

    def __init__(self, t, name):
        self.t = t
        self.b = Buf(name)


class Emitter:
    def __init__(self, nc, stack):
        self.nc = nc
        self.eng = {"pe": nc.tensor, "act": nc.scalar, "dve": nc.vector,
                    "pool": nc.gpsimd, "sp": nc.sync}
        self.cnt = {e: 0 for e in ("pe", "act", "dve", "pool")}
        self.sems = {e: [stack.enter_context(nc.semaphore(f"s_{e}{i}")) for i in range(N_EPOCH)]
                     for e in ("pe", "act", "dve", "pool")}
        self.dsem = [stack.enter_context(nc.semaphore(f"s_dma{i}")) for i in range(N_DMA_SEM)]
        self.dval = [0] * N_DMA_SEM
        self.dnext2 = [0, 0]
        self.waited = {e: {} for e in self.eng}

    def _wait(self, e, ref):
        if ref[0] == "eng":
            _, pe_, n = ref
            key = ("eng", pe_)
            if self.waited[e].get(key, 0) >= n:
                return
            self.waited[e][key] = n
            self.eng[e].wait_ge(self.sems[pe_][(n - 1) // EPOCH], (n - 1) % EPOCH + 1)
        else:
            _, si, val = ref
            key = ("dma", si)
            if self.waited[e].get(key, 0) >= val:
                return
            self.waited[e][key] = val
            self.eng[e].wait_ge(self.dsem[si], val)

    def _deps(self, e, reads, writes, is_dma=False):
        deps = []
        for b in list(reads) + list(writes):
            if b.w is not None:
                deps.append(b.w)
        for b in writes:
            deps.extend(b.r)
        for ref in deps:
            if ref[0] == "eng" and ref[1] == e and e == "pe" and not is_dma:
                continue
            self._wait(e, ref)

    def op(self, e, fn, reads=(), writes=(), inc=True):
        self._deps(e, reads, writes)
        ins = fn()
        n = self.cnt[e] + 1
        ref = ("eng", e, n)
        if inc:
            self.cnt[e] = n
            ins.then_inc(self.sems[e][(n - 1) // EPOCH], 1)
        for b in reads:
            b.r.append(ref)
            if len(b.r) > 64:
                b.r = b.r[-64:] if False else b.r
        for b in writes:
            b.w = ref
            b.r = []
        return ins

    def dma(self, q, out, in_, reads=(), writes=(), **kw):
        half = N_DMA_SEM // 2
        k = 1 if q == "pool" else 0
        si = k * half + self.dnext2[k]
        self.dnext2[k] = (self.dnext2[k] + 1) % half
        if self.dval[si] > 0:
            self._wait(q, ("dma", si, self.dval[si]))
        self._deps(q, reads, writes, is_dma=True)
        self.dval[si] += 16
        ref = ("dma", si, self.dval[si])
        self.eng[q].dma_start(out=out, in_=in_, **kw).then_inc(self.dsem[si], 16)
        for b in reads:
            b.r.append(ref)
        for b in writes:
            b.w = ref
            b.r = []
        return ref

    def barrier(self):
        for e in self.eng:
            for pe_ in self.cnt:
                if pe_ != e and self.cnt[pe_] > 0:
                    self._wait(e, ("eng", pe_, self.cnt[pe_]))
            for si in range(N_DMA_SEM):
                if self.dval[si] > 0:
                    self._wait(e, ("dma", si, self.dval[si]))


C_RQ, C_RK, C_RV, C_RG, C_MQ, C_MK, C_MV, C_GA, C_GB = 0, 512, 1024, 2048, 3072, 3584, 4096, 4608, 5632


def granule_table():
    g = []
    A = lambda src, cols, perm=False: g.append(("A", src, cols, perm))
    B = lambda src, c0, k0=0, nk=8: g.append(("B", src, c0, k0, nk))
    A("w_in", [C_RK + 128 * h for h in range(4)])
    A("w_in", [C_RK + 128 * h for h in range(4)], True)
    A("w_in", [C_MK + 128 * u for u in range(4)])
    B("w_in", C_RV); B("w_in", C_RV + 512)
    B("w_in", C_MV)
    A("w_in", [C_RQ + 128 * h for h in range(4)])
    A("w_in", [C_RQ + 128 * h for h in range(4)], True)
    A("w_in", [C_MQ + 128 * u for u in range(4)])
    A("w_in", [C_GA + 128 * u for u in range(4)]); A("w_in", [C_GA + 512 + 128 * u for u in range(4)])
    A("w_in", [C_GB + 128 * u for u in range(4)]); A("w_in", [C_GB + 512 + 128 * u for u in range(4)])
    B("w_in", C_RG); B("w_in", C_RG + 512)
    A("w_ret_out", [128 * u for u in range(4)]); A("w_ret_out", [512 + 128 * u for u in range(4)])
    g.append(("C", "w_moba_out", [128 * u for u in range(4)]))
    g.append(("C", "w_moba_out", [512 + 128 * u for u in range(4)]))
    B("w_o", 0); B("w_o", 512)
    for i in range(6):
        A("w_ffn_gate", [128 * u for u in range(4 * i, min(4 * i + 4, NKC_F))])
    for i in range(6):
        A("w_ffn_up", [128 * u for u in range(4 * i, min(4 * i + 4, NKC_F))])
    for hf in range(2):
        for k0, nk in ((0, 8), (8, 8), (16, 6)):
            B("w_ffn_down", hf * 512, k0, nk)
    return g


G_CHUNK = [float(np.exp(128.0 * np.log1p(-np.exp2(-5.0 - h)))) for h in range(4)]
GRAN = granule_table()
NG = len(GRAN)
G_GATE, G_UP, G_DOWN = 21, 27, 33


def build(dbg=False):
    nc = bass.Bass("TRN2", target_bir_lowering=False)
    IN = lambda name, shape, dt=F32: nc.dram_tensor(name, shape, dt, kind="ExternalInput").ap()
    SCR = lambda name, shape, dt=BF16: nc.dram_tensor(
        name, shape, dt, kind=("ExternalOutput" if dbg else "Internal")).ap()
    xv = IN("xv", [SV, D])
    wsrc = {"w_in": IN("w_in", [D, IN_COLS]), "w_ret_out": IN("w_ret_out", [D, D]),
            "w_moba_out": IN("w_moba_out", [512, D]), "w_o": IN("w_o", [D, D]),
            "w_ffn_gate": IN("w_ffn_gate", [D, FFN]), "w_ffn_up": IN("w_ffn_up", [D, FFN]),
            "w_ffn_down": IN("w_ffn_down", [FFN, D])}
    n1w = IN("norm1_w", [1, D]); n2w = IN("norm2_w", [1, D])
    qnw = IN("q_norm_w", [64, 1]); knw = IN("k_norm_w", [64, 1])
    cosk_d = IN("cosk", [128, SV]); sink_d = IN("sink", [128, SV])
    cosq_d = IN("cosq", [128, S_HALF]); sinq_d = IN("sinq", [128, S_HALF])
    gbias_d = IN("gbias", [S_HALF, 32]); ownsel_d = IN("ownsel", [S_HALF, 32])
    dmat_d = IN("dmatT", [128, 512]); xi_d = IN("xitab", [128, 512]); zeta_d = IN("zetatab", [128, 512])
    ident_d = IN("ident", [128, 128], BF16); cm_d = IN("cmask", [128, 512], BF16)
    eoh_d = IN("eonehot", [32, SV], BF16)
    out_d = nc.dram_tensor("out", [S_HALF, D], F32, kind="ExternalOutput").ap()

    Wg = nc.dram_tensor("Wg", [NG, 128, 4096], BF16, kind="Internal").ap()
    KT = SCR("KT", [8, 64, SV])
    QS = SCR("QS", [8, 96, S_HALF])
    OT = SCR("OT", [8, 64, S_HALF])
    MIXA = SCR("MIXA", [128, 8, S_HALF])
    SGB = SCR("SGB", [128, 8, S_HALF])
    B_Wg = [Buf(f"Wg{i}") for i in range(NG)]
    B_KT = [Buf(f"KT{h}") for h in range(8)]
    B_QS = [Buf(f"QS{h}") for h in range(8)]
    B_OT = Buf("OT"); B_MIXA = Buf("MIXA"); B_SGB = Buf("SGB"); B_out = Buf("out")

    with ExitStack() as st0:
        em = Emitter(nc, st0)
        op = em.op

        def SB(st, name, shape, dt):
            return TB(st.enter_context(nc.sbuf_tensor("sb_" + name, shape, dt)), name)

        def PS(st, name, shape, dt):
            return TB(st.enter_context(nc.psum_tensor("ps_" + name, shape, dt)), name)

        psF2 = [PS(st0, f"psF2_{i}", [128, 1024], F32) for i in range(3)]
        psF = [TB(psF2[i // 2].t[:, (i % 2) * 512:(i % 2 + 1) * 512], f"psF{i}") for i in range(6)]
        psB = [PS(st0, f"psB{i}", [128, 1024], BF16) for i in range(2)]
        pfi = [0]; pbi = [0]

        def pf(n=4):
            i = pfi[0] % n
            pfi[0] += 1
            return psF[i]

        def pb():
            i = pbi[0] % 2
            pbi[0] += 1
            return psB[i]

        wslot = [SB(st0, f"wslot{i}", [128, 4096], BF16) for i in range(NSLOT)]
        ident = SB(st0, "ident", [128, 128], BF16)
        onesf = SB(st0, "onesf", [128, 64], F32)
        onesb = SB(st0, "onesb", [64, 64], BF16)
        wn1 = SB(st0, "wn1", [128, D], F32)
        gq = SB(st0, "gq", [64, 1], F32)
        gk = SB(st0, "gk", [64, 1], F32)
        gk2 = SB(st0, "gk2", [128, 1], F32)
        onesblk = SB(st0, "onesblk", [128, 128], BF16)

        def cast_granule(gi):
            gd = GRAN[gi]
            W = wsrc[gd[1]]
            if gd[0] == "A":
                dst = Wg[gi].rearrange("p (kc c) -> p kc c", kc=8)
                if gd[3]:
                    for u, c0 in enumerate(gd[2]):
                        for (d0, s0) in ((0, c0 + 64), (64, c0)):
                            em.dma("pool", dst[:, :, u * 128 + d0:u * 128 + d0 + 64],
                                   W[:, s0:s0 + 64].rearrange("(kc p) c -> p kc c", p=128), writes=[B_Wg[gi]])
                else:
                    c0 = gd[2][0]
                    ncol = 128 * len(gd[2])
                    em.dma("pool", dst[:, :, 0:ncol], W[:, c0:c0 + ncol].rearrange("(kc p) c -> p kc c", p=128),
                           writes=[B_Wg[gi]])
            elif gd[0] == "B":
                _, _, c0, k0, nk = gd
                dst = Wg[gi].rearrange("p (kc c) -> p kc c", kc=8)
                em.dma("pool", dst[:, 0:nk],
                       W[k0 * 128:(k0 + nk) * 128, c0:c0 + 512].rearrange("(kc p) c -> p kc c", p=128),
                       writes=[B_Wg[gi]])
            else:
                dst = Wg[gi, 0:64].rearrange("p (h c) -> p h c", h=8)
                c0 = gd[2][0]
                em.dma("pool", dst, W[:, c0:c0 + 512].rearrange("(h p) c -> p h c", p=64), writes=[B_Wg[gi]])

        for gi in range(6):
            cast_granule(gi)
        cast_next = [6]

        def cast_more(k):
            for _ in range(k):
                if cast_next[0] < NG:
                    cast_granule(cast_next[0])
                    cast_next[0] += 1

        em.dma("sp", ident.t[:], ident_d, writes=[ident.b])
        em.dma("sp", wn1.t[:], n1w.partition_broadcast(128), writes=[wn1.b])
        em.dma("sp", gq.t[:], qnw, writes=[gq.b])
        em.dma("sp", gk.t[:], knw, writes=[gk.b])
        em.dma("sp", gk2.t[0:64, :], knw, writes=[gk2.b])
        em.dma("sp", gk2.t[64:128, :], knw, writes=[gk2.b])
        op("pool", lambda: nc.gpsimd.memset(onesblk.t[:], 0.0), writes=[onesblk.b])
        op("pool", lambda: nc.gpsimd.memset(onesblk.t[0:64, 0:64], 1.0), writes=[onesblk.b])
        op("pool", lambda: nc.gpsimd.memset(onesblk.t[64:128, 64:128], 1.0), writes=[onesblk.b])
        op("pool", lambda: nc.gpsimd.memset(onesf.t[:], 1.0), writes=[onesf.b])
        op("pool", lambda: nc.gpsimd.memset(onesb.t[:], 1.0), writes=[onesb.b])

        seq1 = []
        for s in range(NSEG):
            if s < 8:
                seq1 += [0, 1, 3, 4, 2, 5]
            else:
                seq1 += [0, 1, 6, 7, 3, 4, 13, 14, 2, 11, 5, 12, 8, 9, 15, 10, 16]
        seqC = []
        sF = [17, 18, 19, 20]
        sG = [x for i in range(6) for x in (G_GATE + i, G_UP + i)]
        sD = list(range(G_DOWN, G_DOWN + 6))
        seqC += sF
        for j in range(8):
            seqC += sG + (sF if j + 1 < 8 else []) + sD
        ws = {"seq": seq1, "slots": wslot, "pos": 0, "emit": 0}

        def wstream(seq, slots):
            ws["seq"] = seq; ws["slots"] = slots; ws["pos"] = 0; ws["emit"] = 0

        def wget(expect):
            seq, slots = ws["seq"], ws["slots"]
            i = ws["pos"]
            assert seq[i] == expect, (i, seq[i], expect)
            ns = len(slots)
            while ws["emit"] < min(len(seq), i + ns - 1):
                k = ws["emit"]
                sl = slots[k % ns]
                gi = seq[k]
                gd = GRAN[gi]
                if gd[0] == "C":
                    em.dma("sp", sl.t[0:64, :], Wg[gi, 0:64, :], reads=[B_Wg[gi]], writes=[sl.b])
                elif gd[0] == "A" and len(gd[2]) < 4:
                    nc_ = 128 * len(gd[2])
                    em.dma("sp", sl.t[:].rearrange("p (kc c) -> p kc c", kc=8)[:, :, 0:nc_],
                           Wg[gi].rearrange("p (kc c) -> p kc c", kc=8)[:, :, 0:nc_], reads=[B_Wg[gi]], writes=[sl.b])
                elif gd[0] == "B" and gd[4] < 8:
                    em.dma("sp", sl.t[:, 0:gd[4] * 512], Wg[gi, :, 0:gd[4] * 512], reads=[B_Wg[gi]], writes=[sl.b])
                else:
                    em.dma("sp", sl.t[:], Wg[gi], reads=[B_Wg[gi]], writes=[sl.b])
                ws["emit"] += 1
            ws["pos"] += 1
            return slots[i % ns]

        vB = lambda sl: sl.t[:].rearrange("p (kc c) -> p kc c", kc=8)
        vC = lambda sl: sl.t[0:64, :].rearrange("p (h c) -> p h c", h=8)

        def mm(out_ap, lhsT, rhs, start, stop, reads, wb, inc):
            op("pe", lambda: nc.tensor.matmul(out_ap, lhsT=lhsT, rhs=rhs, start=start, stop=stop),
               reads=reads, writes=[wb], inc=inc)

        def proj_fm(sl, u, hT, M=128, c0=0):
            p = pf()
            w = vB(sl)
            for kc in range(8):
                mm(p.t[0:M, :], w[:, kc, u * 128 + c0:u * 128 + c0 + M], hT.t[:, kc, :], kc == 0, kc == 7,
                   [sl.b, hT.b], p.b, kc == 7)
            return p

        def proj_tm(sl, hT, t):
            p = pf()
            w = vB(sl)
            for kc in range(8):
                mm(p.t[:, :], hT.t[:, kc, t * 128:(t + 1) * 128], w[:, kc, :], kc == 0, kc == 7,
                   [sl.b, hT.b], p.b, kc == 7)
            return p

        def rstd_small(ss, n, cols, eps):
            op("act", lambda: nc.scalar.activation(out=ss.t[:, cols], in_=ss.t[:, cols], func=AF.Ln,
                                                   bias=eps, scale=1.0 / n), reads=[ss.b], writes=[ss.b])
            op("act", lambda: nc.scalar.activation(out=ss.t[:, cols], in_=ss.t[:, cols], func=AF.Exp,
                                                   scale=-0.5), reads=[ss.b], writes=[ss.b])

        def norm_T(xt_ap, xb, wn, ss, sqj, xs, hT, t):
            op("act", lambda: nc.scalar.activation(out=sqj.t[:], in_=xt_ap, func=AF.Square,
                                                   accum_out=ss.t[:, 0:1]),
               reads=[xb], writes=[sqj.b, ss.b])
            rstd_small(ss, D, slice(0, 1), 1e-6)
            op("dve", lambda: nc.vector.scalar_tensor_tensor(out=xs.t[:], in0=xt_ap, scalar=ss.t[:, 0:1],
                                                             in1=wn.t[:], op0=ALU.mult, op1=ALU.mult),
               reads=[xb, ss.b, wn.b], writes=[xs.b])
            pt = pb()
            for kc in range(8):
                op("pe", lambda kc=kc: nc.tensor.transpose(out=pt.t[:, kc * 128:(kc + 1) * 128],
                                                           in_=xs.t[:, kc * 128:(kc + 1) * 128],
                                                           identity=ident.t[:]),
                   reads=[xs.b, ident.b], writes=[pt.b], inc=(kc == 7))
            op("act", lambda: nc.scalar.copy(out=hT.t[:, :, t * 128:(t + 1) * 128],
                                             in_=pt.t[:].rearrange("p (k c) -> p k c", k=8)),
               reads=[pt.b], writes=[hT.b])

        with ExitStack() as stV:
            V_all = SB(stV, "V_all", [128, 64, 8, 66], BF16)
            op("pool", lambda: nc.gpsimd.memset(V_all.t[:], 1.0), writes=[V_all.b])

            with ExitStack() as st1:
                xt = [SB(st1, f"xt{i}", [128, D], F32) for i in range(2)]
                xs = SB(st1, "xs", [128, D], BF16)
                ss = SB(st1, "ss", [128, 1], F32)
                hTs = [SB(st1, f"hT{i}", [128, 8, SEG], BF16) for i in range(2)]
                ck = SB(st1, "ck", [128, SEG], F32); sk = SB(st1, "sk", [128, SEG], F32)
                tmpA = SB(st1, "tmpA", [128, SEG], F32); tmpB = SB(st1, "tmpB", [128, SEG], F32)
                rkT = SB(st1, "rkT", [128, 4, SEG], BF16)
                rqT = SB(st1, "rqT", [128, 4, SEG], BF16)
                rqxT = SB(st1, "rqxT", [128, 4, SEG], BF16)
                rvS = SB(st1, "rvS", [128, 4, D], BF16)
                sg = SB(st1, "sg", [128, 4, D], BF16)
                kz = SB(st1, "kz", [128, 512], BF16)
                STt = SB(st1, "STt", [128, 512], BF16)
                ons = [SB(st1, f"on{i}", [128, 256], F32) for i in range(2)]
                ret = SB(st1, "ret", [128, D], BF16)
                retT = SB(st1, "retT", [128, 8, SEG], BF16)
                sqm = SB(st1, "sqm", [128, SEG], BF16)
                rs = SB(st1, "rs", [128, SEG], F32)
                kTs = [SB(st1, f"kTs{i}", [128, SEG], BF16) for i in range(2)]
                ksum = SB(st1, "ksum", [128, 2], F32)
                kb2 = SB(st1, "kb2", [128, 2], BF16)
                kbarT = SB(st1, "kbarT", [64, 8, 32], BF16)
                QA = SB(st1, "QA", [96, 8, SEG], BF16)
                Z = SB(st1, "Z", [128, 8, 96], BF16)
                g3 = SB(st1, "g3", [128, 8, 32], F32)
                m3 = SB(st1, "m3", [128, 8, 32], F32)
                v3 = SB(st1, "v3", [128, 8, 32], F32)
                mx = SB(st1, "mx", [128, 8, 8], F32)
                gbt = SB(st1, "gbt", [128, 32], F32); ost = SB(st1, "ost", [128, 32], F32)
                gst = SB(st1, "gst", [128, 12], F32)
                gstb = [Buf(f"gst{i}") for i in range(9)]
                sga = [SB(st1, f"sga{i}", [128, SEG], BF16) for i in range(2)]
                stg = [SB(st1, f"stg{i}", [128, SEG], BF16) for i in range(2)]
                Sst = SB(st1, "Sst", [128, 1024], F32)
                Sbf = SB(st1, "Sbf", [128, 1024], BF16)
                dmat = SB(st1, "dmat", [128, 512], F32)
                xit = SB(st1, "xit", [128, 512], F32)
                zet = SB(st1, "zet", [128, 512], F32)
                for tb, src in ((dmat, dmat_d), (xit, xi_d), (zet, zeta_d)):
                    em.dma("sp", tb.t[:], src, writes=[tb.b])
                op("pool", lambda: nc.gpsimd.memset(Sst.t[:], 0.0), writes=[Sst.b])
                op("pool", lambda: nc.gpsimd.memset(Sbf.t[:], 0.0), writes=[Sbf.b])
                op("pool", lambda: nc.gpsimd.memset(kbarT.t[:], 0.0), writes=[kbarT.b])
                op("pool", lambda: nc.gpsimd.memset(kb2.t[:], 0.0), writes=[kb2.b])
                op("pool", lambda: nc.gpsimd.memset(Z.t[:], 0.0), writes=[Z.b])
                xload = [0]

                def load_x(tile_idx):
                    tb = xt[tile_idx % 2]
                    em.dma("sp", tb.t[:], xv[tile_idx * 128:(tile_idx + 1) * 128, :], writes=[tb.b])

                load_x(0)

                def emit_norm(s_):
                    for t in range(4):
                        ti = s_ * 4 + t
                        if ti + 1 < NSEG * 4:
                            load_x(ti + 1)
                        norm_T(xt[ti % 2].t[:], xt[ti % 2].b, wn1, ss, xs, xs, hTs[s_ % 2], t)

                def rotary(p1, p2, ct, st_, dst, h):
                    op("dve", lambda: nc.vector.tensor_tensor(out=tmpA.t[:], in0=p1.t[:], in1=ct.t[:], op=ALU.mult),
                       reads=[p1.b, ct.b], writes=[tmpA.b])
                    op("dve", lambda: nc.vector.tensor_tensor(out=tmpB.t[:], in0=p2.t[:], in1=st_.t[:], op=ALU.mult),
                       reads=[p2.b, st_.b], writes=[tmpB.b])
                    op("dve", lambda: nc.vector.tensor_tensor(out=dst.t[:, h, :], in0=tmpA.t[:], in1=tmpB.t[:],
                                                              op=ALU.add),
                       reads=[tmpA.b, tmpB.b], writes=[dst.b])

                def qknorm(p, gw, dst_ap, dst_b):
                    op("act", lambda: nc.scalar.activation(out=sqm.t[0:64, :], in_=p.t[0:64, :], func=AF.Square),
                       reads=[p.b], writes=[sqm.b])
                    p2 = pf()
                    mm(p2.t[0:64, :], onesb.t[:], sqm.t[0:64, :], True, True, [onesb.b, sqm.b], p2.b, True)
                    op("act", lambda: nc.scalar.activation(out=rs.t[0:64, :], in_=p2.t[0:64, :], func=AF.Ln,
                                                           bias=1e-6, scale=1.0 / 64),
                       reads=[p2.b], writes=[rs.b])
                    op("act", lambda: nc.scalar.activation(out=rs.t[0:64, :], in_=rs.t[0:64, :], func=AF.Exp, scale=-0.5),
                       reads=[rs.b], writes=[rs.b])
                    op("dve", lambda: nc.vector.scalar_tensor_tensor(out=dst_ap, in0=p.t[0:64, :], scalar=gw.t[:, 0:1],
                                                                     in1=rs.t[0:64, :], op0=ALU.mult, op1=ALU.mult),
                       reads=[p.b, gw.b, rs.b], writes=[dst_b])

                def normA(s_, t):
                    ti = s_ * 4 + t
                    if ti + 1 < NSEG * 4:
                        load_x(ti + 1)
                    xtb = xt[ti % 2]
                    op("act", lambda: nc.scalar.activation(out=xs.t[:], in_=xtb.t[:], func=AF.Square,
                                                           accum_out=ss.t[:, 0:1]),
                       reads=[xtb.b], writes=[xs.b, ss.b])
                    rstd_small(ss, D, slice(0, 1), 1e-6)
                    op("dve", lambda: nc.vector.scalar_tensor_tensor(out=xs.t[:], in0=xtb.t[:], scalar=ss.t[:, 0:1],
                                                                     in1=wn1.t[:], op0=ALU.mult, op1=ALU.mult),
                       reads=[xtb.b, ss.b, wn1.b], writes=[xs.b])

                def normB(s_, t):
                    pt = pb()
                    for kc in range(8):
                        op("pe", lambda kc=kc: nc.tensor.transpose(out=pt.t[:, kc * 128:(kc + 1) * 128],
                                                                   in_=xs.t[:, kc * 128:(kc + 1) * 128],
                                                                   identity=ident.t[:]),
                           reads=[xs.b, ident.b], writes=[pt.b], inc=(kc == 7))
                    op("act", lambda: nc.scalar.copy(out=hTs[s_ % 2].t[:, :, t * 128:(t + 1) * 128],
                                                     in_=pt.t[:].rearrange("p (k c) -> p k c", k=8)),
                       reads=[pt.b], writes=[hTs[s_ % 2].b])

                emit_norm(0)
                for s in range(NSEG):
                    own = s >= 8
                    j = s - 8
                    cs = slice(s * SEG, (s + 1) * SEG)
                    cast_more(3)
                    em.dma("sp", ck.t[:], cosk_d[:, cs], writes=[ck.b])
                    em.dma("sp", sk.t[:], sink_d[:, cs], writes=[sk.b])
                    hT = hTs[s % 2]
                    s0 = wget(0); s1 = wget(1)
                    for h in range(4):
                        rotary(proj_fm(s0, h, hT), proj_fm(s1, h, hT), ck, sk, rkT, h)
                    if own:
                        em.dma("sp", ck.t[:], cosq_d[:, j * SEG:(j + 1) * SEG], writes=[ck.b])
                        em.dma("sp", sk.t[:], sinq_d[:, j * SEG:(j + 1) * SEG], writes=[sk.b])
                        s0 = wget(6); s1 = wget(7)
                        for h in range(4):
                            rotary(proj_fm(s0, h, hT), proj_fm(s1, h, hT), ck, sk, rqT, h)
                            op("pool", lambda h=h: nc.gpsimd.tensor_tensor(
                                out=rqxT.t[:, h, :].rearrange("p (c n) -> p c n", c=4),
                                in0=rqT.t[:, h, :].rearrange("p (c n) -> p c n", c=4),
                                in1=xit.t[:, None, h * 128:(h + 1) * 128].to_broadcast([128, 4, 128]), op=ALU.mult),
                               reads=[rqT.b, xit.b], writes=[rqxT.b])
                    sv = [wget(3), wget(4)]
                    for t in range(4):
                        for hf in range(2):
                            p = proj_tm(sv[hf], hT, t)
                            op("act", lambda p=p, t=t, hf=hf: nc.scalar.copy(
                                out=rvS.t[:, t, hf * 512:(hf + 1) * 512], in_=p.t[:]),
                               reads=[p.b], writes=[rvS.b])
                    if own:
                        sgw = [wget(13), wget(14)]
                        for t in range(4):
                            for hf in range(2):
                                p = proj_tm(sgw[hf], hT, t)
                                op("act", lambda p=p, t=t, hf=hf: nc.scalar.activation(
                                    out=sg.t[:, t, hf * 512:(hf + 1) * 512], in_=p.t[:], func=AF.Silu),
                                   reads=[p.b], writes=[sg.b])
                    pos_all = {}

                    def chunk_s1(c):
                        cc = slice(c * 128, (c + 1) * 128)
                        pos_ = []
                        pos_all[c] = pos_
                        if own:
                            psc = pf()
                            for h in range(4):
                                mm(psc.t[:, h * 128:(h + 1) * 128], rkT.t[:, h, cc], rqT.t[:, h, cc], True, True,
                                   [rkT.b, rqT.b], psc.b, h == 3)
                            op("dve", lambda psc=psc: nc.vector.tensor_tensor(out=STt.t[:], in0=psc.t[:], in1=dmat.t[:],
                                                                             op=ALU.mult),
                               reads=[psc.b, dmat.b], writes=[STt.b])
                            for hp in range(2):
                                po = psF[4 + hp]
                                pos_.append(po)
                                for hh in range(2):
                                    h = hp * 2 + hh
                                    mm(po.t[:, hh * 256:(hh + 1) * 256], STt.t[:, h * 128:(h + 1) * 128],
                                       rvS.t[:, c, h * 256:(h + 1) * 256], True, False, [STt.b, rvS.b], po.b, False)
                                    mm(po.t[:, hh * 256:(hh + 1) * 256], rqxT.t[:, h, cc],
                                       Sbf.t[:, h * 256:(h + 1) * 256], False, True, [rqxT.b, Sbf.b], po.b, hh == 1)
                        pt = pb()
                        for h in range(4):
                            op("pe", lambda h=h, pt=pt, cc=cc: nc.tensor.transpose(
                                out=pt.t[:, h * 128:(h + 1) * 128], in_=rkT.t[:, h, cc], identity=ident.t[:]),
                               reads=[rkT.b, ident.b], writes=[pt.b], inc=(h == 3))
                        op("dve", lambda pt=pt: nc.vector.tensor_tensor(out=kz.t[:], in0=pt.t[:, 0:512], in1=zet.t[:],
                                                                       op=ALU.mult),
                           reads=[pt.b, zet.b], writes=[kz.b])
                        for hp in range(2):
                            pu = pf()
                            for hh in range(2):
                                h = hp * 2 + hh
                                mm(pu.t[:, hh * 256:(hh + 1) * 256], kz.t[:, h * 128:(h + 1) * 128],
                                   rvS.t[:, c, h * 256:(h + 1) * 256], True, True, [kz.b, rvS.b], pu.b, hh == 1)
                            for hh in range(2):
                                h = hp * 2 + hh
                                hs = slice(h * 256, (h + 1) * 256)
                                op("dve", lambda hs=hs, pu=pu, hh=hh, h=h: nc.vector.scalar_tensor_tensor(
                                    out=Sst.t[:, hs], in0=Sst.t[:, hs], scalar=float(G_CHUNK[h]),
                                    in1=pu.t[:, hh * 256:(hh + 1) * 256], op0=ALU.mult, op1=ALU.add),
                                   reads=[Sst.b, pu.b], writes=[Sst.b])
                        op("act", lambda: nc.scalar.copy(out=Sbf.t[:], in_=Sst.t[:]), reads=[Sst.b], writes=[Sbf.b])

                    def chunk_s2(c):
                        cc = slice(c * 128, (c + 1) * 128)
                        pos_ = pos_all[c]
                        if own:
                            for h in range(4):
                                oap = pos_[h // 2].t[:, (h % 2) * 256:(h % 2 + 1) * 256]
                                op("act", lambda oap=oap, h=h: nc.scalar.activation(
                                    out=ret.t[:, h * 256:(h + 1) * 256], in_=oap, func=AF.Identity,
                                    accum_out=gst.t[:, h:h + 1]), reads=[pos_[h // 2].b], writes=[gstb[h], ret.b])
                                op("act", lambda oap=oap, h=h: nc.scalar.activation(
                                    out=ret.t[:, h * 256:(h + 1) * 256], in_=oap, func=AF.Square,
                                    accum_out=gst.t[:, 4 + h:5 + h]), reads=[pos_[h // 2].b], writes=[gstb[4 + h], ret.b])
                            op("dve", lambda: nc.vector.tensor_scalar(
                                out=gst.t[:, 0:4], in0=gst.t[:, 0:4], scalar1=1.0 / 256, scalar2=None, op0=ALU.mult),
                               reads=gstb[0:4], writes=gstb[0:4])
                            op("dve", lambda: nc.vector.tensor_tensor(
                                out=gst.t[:, 8:12], in0=gst.t[:, 0:4], in1=gst.t[:, 0:4], op=ALU.mult),
                               reads=gstb[0:4], writes=[gstb[8]])
                            op("dve", lambda: nc.vector.scalar_tensor_tensor(
                                out=gst.t[:, 4:8], in0=gst.t[:, 4:8], scalar=1.0 / 256, in1=gst.t[:, 8:12],
                                op0=ALU.mult, op1=ALU.subtract), reads=gstb[4:9], writes=gstb[4:8])
                            op("act", lambda: nc.scalar.activation(out=gst.t[:, 4:8], in_=gst.t[:, 4:8], func=AF.Ln,
                                                                   bias=1e-5, scale=1.0),
                               reads=gstb[4:8], writes=gstb[4:8])
                            op("act", lambda: nc.scalar.activation(out=gst.t[:, 4:8], in_=gst.t[:, 4:8], func=AF.Exp,
                                                                   scale=-0.5),
                               reads=gstb[4:8], writes=gstb[4:8])
                            op("dve", lambda: nc.vector.scalar_tensor_tensor(
                                out=gst.t[:, 8:12], in0=gst.t[:, 0:4], scalar=-1.0, in1=gst.t[:, 4:8],
                                op0=ALU.mult, op1=ALU.mult), reads=gstb[0:8], writes=[gstb[8]])
                            for h in range(4):
                                oap = pos_[h // 2].t[:, (h % 2) * 256:(h % 2 + 1) * 256]
                                onb = ons[h % 2]
                                op("act", lambda oap=oap, h=h, onb=onb: nc.scalar.activation(
                                    out=onb.t[:], in_=oap, func=AF.Identity, scale=gst.t[:, 4 + h:5 + h],
                                    bias=gst.t[:, 8 + h:9 + h]), reads=[pos_[h // 2].b] + gstb[0:9], writes=[onb.b])
                                op("dve", lambda h=h, c=c, onb=onb: nc.vector.tensor_tensor(
                                    out=ret.t[:, h * 256:(h + 1) * 256], in0=onb.t[:],
                                    in1=sg.t[:, c, h * 256:(h + 1) * 256], op=ALU.mult),
                                   reads=[onb.b, sg.b], writes=[ret.b])
                            pt = pb()
                            for kc in range(8):
                                op("pe", lambda kc=kc, pt=pt: nc.tensor.transpose(
                                    out=pt.t[:, kc * 128:(kc + 1) * 128], in_=ret.t[:, kc * 128:(kc + 1) * 128],
                                    identity=ident.t[:]), reads=[ret.b, ident.b], writes=[pt.b], inc=(kc == 7))
                            op("act", lambda pt=pt, cc=cc: nc.scalar.copy(
                                out=retT.t[:, :, cc], in_=pt.t[:].rearrange("p (k c) -> p k c", k=8)),
                               reads=[pt.b], writes=[retT.b])

                    def mk_pair(u, s2_):
                        p = proj_fm(s2_, u, hT)
                        kst = kTs[u % 2]
                        op("act", lambda: nc.scalar.activation(out=sqm.t[:], in_=p.t[:], func=AF.Square),
                           reads=[p.b], writes=[sqm.b])
                        p2 = pf()
                        mm(p2.t[:], onesblk.t[:], sqm.t[:], True, True, [onesblk.b, sqm.b], p2.b, True)
                        op("act", lambda: nc.scalar.activation(out=rs.t[:], in_=p2.t[:], func=AF.Ln, bias=1e-6,
                                                               scale=1.0 / 64), reads=[p2.b], writes=[rs.b])
                        op("act", lambda: nc.scalar.activation(out=rs.t[:], in_=rs.t[:], func=AF.Exp, scale=-0.5),
                           reads=[rs.b], writes=[rs.b])
                        op("dve", lambda: nc.vector.scalar_tensor_tensor(out=kst.t[:], in0=p.t[:], scalar=gk2.t[:, 0:1],
                                                                         in1=rs.t[:], op0=ALU.mult, op1=ALU.mult),
                           reads=[p.b, gk2.b, rs.b], writes=[kst.b])
                        op("dve", lambda: nc.vector.reduce_sum(
                            out=ksum.t[:], in_=kst.t[:].rearrange("p (b k) -> p b k", b=2), axis=AX.X),
                           reads=[kst.b], writes=[ksum.b])
                        op("dve", lambda: nc.vector.tensor_scalar(
                            out=kbarT.t[:, 2 * u, 2 * s:2 * s + 2], in0=ksum.t[0:64, :], scalar1=1.0 / 256, scalar2=None,
                            op0=ALU.mult), reads=[ksum.b], writes=[kbarT.b])
                        op("dve", lambda: nc.vector.tensor_scalar(
                            out=kb2.t[64:128, :], in0=ksum.t[64:128, :], scalar1=1.0 / 256, scalar2=None,
                            op0=ALU.mult), reads=[ksum.b], writes=[kb2.b])
                        pk = pf()
                        mm(pk.t[0:64, 0:2], ident.t[:, 64:128], kb2.t[:], True, True, [ident.b, kb2.b], pk.b, True)
                        op("act", lambda: nc.scalar.copy(out=kbarT.t[:, 2 * u + 1, 2 * s:2 * s + 2], in_=pk.t[0:64, 0:2]),
                           reads=[pk.b], writes=[kbarT.b])
                        em.dma("pool", KT[2 * u:2 * u + 2, :, cs].rearrange("h p c -> (h p) c"), kst.t[:],
                               reads=[kst.b], writes=[B_KT[2 * u], B_KT[2 * u + 1]])

                    def mv_step(s5_):
                        for t in range(4):
                            p = proj_tm(s5_, hT, t)
                            op("act", lambda p=p, t=t, s=s: nc.scalar.copy(
                                out=V_all.t[:, s * 4 + t, :, 0:64], in_=p.t[:].rearrange("p (h d) -> p h d", h=8)),
                               reads=[p.b], writes=[V_all.b])

                    def mq_step(h, s8_):
                        u, e = h // 2, h % 2
                        p = proj_fm(s8_, u, hT, M=64, c0=64 * e)
                        qknorm(p, gq, QA.t[0:64, h, :], QA.b)

                    def gate_s1(t):
                        qs = slice(j * SEG + t * 128, j * SEG + (t + 1) * 128)
                        em.dma("sp", gbt.t[:], gbias_d[qs, :], writes=[gbt.b])
                        em.dma("sp", ost.t[:], ownsel_d[qs, :], writes=[ost.b])
                        pg = pf()
                        for h in range(8):
                            mm(pg.t[:, h * 32:(h + 1) * 32], QA.t[0:64, h, t * 128:(t + 1) * 128], kbarT.t[:, h, :],
                               True, True, [QA.b, kbarT.b], pg.b, h == 7)
                        op("dve", lambda pg=pg: nc.vector.tensor_tensor(
                            out=g3.t[:], in0=pg.t[:, 0:256].rearrange("p (h n) -> p h n", h=8),
                            in1=gbt.t[:, None, :].to_broadcast([128, 8, 32]), op=ALU.add),
                           reads=[pg.b, gbt.b], writes=[g3.b])
                        for h in range(8):
                            op("dve", lambda h=h: nc.vector.max(out=mx.t[:, h, :], in_=g3.t[:, h, :]),
                               reads=[g3.b], writes=[mx.b])
                        op("dve", lambda: nc.vector.tensor_tensor(
                            out=m3.t[:], in0=g3.t[:], in1=mx.t[:, :, 2:3].to_broadcast([128, 8, 32]), op=ALU.is_ge),
                           reads=[g3.b, mx.b], writes=[m3.b])
                        op("dve", lambda: nc.vector.tensor_single_scalar(out=v3.t[:], in_=g3.t[:], scalar=-1e29,
                                                                         op=ALU.is_gt),
                           reads=[g3.b], writes=[v3.b])
                        op("dve", lambda: nc.vector.tensor_tensor(out=m3.t[:], in0=m3.t[:], in1=v3.t[:], op=ALU.mult),
                           reads=[m3.b, v3.b], writes=[m3.b])
                        op("dve", lambda: nc.vector.tensor_tensor(
                            out=m3.t[:], in0=m3.t[:], in1=ost.t[:, None, :].to_broadcast([128, 8, 32]), op=ALU.add),
                           reads=[m3.b, ost.b], writes=[m3.b])
                        op("dve", lambda: nc.vector.tensor_scalar(
                            out=Z.t[:, :, 64:96], in0=m3.t[:], scalar1=-1.0, scalar2=NEGB, op0=ALU.add, op1=ALU.mult),
                           reads=[m3.b], writes=[Z.b])

                    def gate_s2(t):
                        pt = pb()
                        for h in range(8):
                            op("pe", lambda h=h, pt=pt: nc.tensor.transpose(
                                out=pt.t[0:96, h * 128:(h + 1) * 128], in_=Z.t[:, h, :], identity=ident.t[:]),
                               reads=[Z.b, ident.b], writes=[pt.b], inc=(h == 7))
                        op("act", lambda pt=pt, t=t: nc.scalar.copy(
                            out=QA.t[64:96, :, t * 128:(t + 1) * 128],
                            in_=pt.t[64:96, :].rearrange("p (h c) -> p h c", h=8)),
                           reads=[pt.b], writes=[QA.b])

                    def fin_a(sa, sr, half, u):
                        uu = half * 4 + u
                        pga = proj_fm(sa, u, hT)
                        sgt = sga[uu % 2]
                        op("act", lambda pga=pga, sgt=sgt: nc.scalar.activation(out=sgt.t[:], in_=pga.t[:],
                                                                                func=AF.Sigmoid),
                           reads=[pga.b], writes=[sgt.b])
                        pa = proj_fm(sr, u, retT)
                        sg_ = stg[uu % 2]
                        op("dve", lambda pa=pa, sgt=sgt, sg_=sg_: nc.vector.tensor_tensor(
                            out=sg_.t[:], in0=pa.t[:], in1=sgt.t[:], op=ALU.mult),
                           reads=[pa.b, sgt.b], writes=[sg_.b])
                        em.dma("pool", MIXA[:, uu, j * SEG:(j + 1) * SEG], sg_.t[:], reads=[sg_.b], writes=[B_MIXA])

                    def fin_b(sb_, half, u):
                        uu = half * 4 + u
                        pgb = proj_fm(sb_, u, hT)
                        sgt = sga[uu % 2]
                        op("act", lambda pgb=pgb, sgt=sgt: nc.scalar.activation(out=sgt.t[:], in_=pgb.t[:],
                                                                                func=AF.Sigmoid),
                           reads=[pgb.b], writes=[sgt.b])
                        em.dma("pool", SGB[:, uu, j * SEG:(j + 1) * SEG], sgt.t[:], reads=[sgt.b], writes=[B_SGB])

                    s2_ = wget(2)
                    N_ = s + 1 < NSEG

                    def nA(t):
                        if N_:
                            normA(s + 1, t)

                    def nB(t):
                        if N_:
                            normB(s + 1, t)

                    nA(0); chunk_s1(0); mk_pair(0, s2_); nB(0); nA(1)
                    if own:
                        chunk_s2(0)
                    mk_pair(1, s2_); nB(1); nA(2)
                    chunk_s1(1); mk_pair(2, s2_); nB(2); nA(3)
                    if own:
                        chunk_s2(1)
                    mk_pair(3, s2_); nB(3)
                    if own:
                        sb_ = wget(11)
                        for u in range(4):
                            fin_b(sb_, 0, u)
                    s5_ = wget(5)
                    chunk_s1(2); mv_step(s5_)
                    if own:
                        chunk_s2(2)
                        sb_ = wget(12)
                        for u in range(4):
                            fin_b(sb_, 1, u)
                    if not own:
                        chunk_s1(3)
                        continue
                    s8_ = wget(8)
                    mq_step(0, s8_); mq_step(1, s8_)
                    chunk_s1(3); mq_step(2, s8_); mq_step(3, s8_)
                    chunk_s2(3)
                    for h in range(4, 8):
                        mq_step(h, s8_)
                    sa = wget(9); sr = wget(15)
                    gate_s1(0); fin_a(sa, sr, 0, 0); fin_a(sa, sr, 0, 1)
                    gate_s2(0); gate_s1(1); fin_a(sa, sr, 0, 2); fin_a(sa, sr, 0, 3)
                    sa = wget(10); sr = wget(16)
                    gate_s2(1); gate_s1(2); fin_a(sa, sr, 1, 0); fin_a(sa, sr, 1, 1)
                    gate_s2(2); gate_s1(3); fin_a(sa, sr, 1, 2); fin_a(sa, sr, 1, 3)
                    gate_s2(3)
                    for h in range(8):
                        em.dma("pool", QS[h, :, j * SEG:(j + 1) * SEG], QA.t[:, h, :], reads=[QA.b], writes=[B_QS[h]])
                cast_more(NG)
                em.barrier()

            with ExitStack() as stB:
                KA = [SB(stB, f"KA{i}", [96, SV], BF16) for i in range(2)]
                QB = [SB(stB, f"QB{i}", [96, S_HALF], BF16) for i in range(2)]
                ptile = [SB(stB, f"ptile{i}", [128, 2 * SEG], BF16) for i in range(3)]
                cm = SB(stB, "cm", [128, 512], BF16)
                rl = SB(stB, "rl", [128, SEG], F32)
                osb = SB(stB, "osb", [64, SEG], F32)
                ot = [SB(stB, f"ot{i}", [64, SEG], BF16) for i in range(2)]
                em.dma("sp", cm.t[:], cm_d, writes=[cm.b])
                for i in range(2):
                    em.dma("sp", KA[i].t[64:96, :], eoh_d, writes=[KA[i].b])
                psO = [psF[4], psF[5]]
                it = 0
                sp_cnt = [0]
                pend_epi = []
                pbc = TB(psB[0].t[:].bitcast(F32), "pbc")
                def loadKQ(h_):
                    em.dma("sp", KA[h_ % 2].t[0:64, :], KT[h_], reads=[B_KT[h_]], writes=[KA[h_ % 2].b])
                    em.dma("sp", QB[h_ % 2].t[:], QS[h_], reads=[B_QS[h_]], writes=[QB[h_ % 2].b])

                loadKQ(0)
                for h in range(8):
                    ka = KA[h % 2]; qb = QB[h % 2]
                    if h + 1 < 8:
                        loadKQ(h + 1)
                    for j in range(8):
                        nkt = 4 * (8 + j + 1)
                        po = psO[it % 2]
                        qsl = slice(j * SEG, (j + 1) * SEG)
                        pi = 0

                        def s_mm(kt):
                            ps_ = pf()
                            mm(ps_.t[:], ka.t[:, kt * 128:(kt + 1) * 128], qb.t[:, qsl], True, True,
                               [ka.b, qb.b], ps_.b, True)
                            return ps_

                        ps_q = [s_mm(k_) for k_ in range(min(2, nkt))]
                        while pend_epi:
                            pend_epi.pop(0)()
                        for kt in range(nkt):
                            ps_cur = ps_q.pop(0)
                            if kt + 2 < nkt:
                                ps_q.append(s_mm(kt + 2))
                            ptb = ptile[pi % 3]; pi += 1
                            op("act", lambda ps_cur=ps_cur, ptb=ptb: nc.scalar.activation(
                                out=ptb.t[:, 0:SEG], in_=ps_cur.t[:], func=AF.Exp, scale=0.125),
                               reads=[ps_cur.b], writes=[ptb.b])
                            dk = kt - (nkt - 4)
                            if dk >= 0:
                                q0 = 0 if dk < 2 else 256
                                mcol = (dk % 2) * 256
                                op("dve", lambda ptb=ptb, q0=q0, mcol=mcol: nc.vector.tensor_tensor(
                                    out=ptb.t[:, q0:q0 + 256], in0=ptb.t[:, q0:q0 + 256],
                                    in1=cm.t[:, mcol:mcol + 256], op=ALU.mult),
                                   reads=[ptb.b, cm.b], writes=[ptb.b])
                            mm(po.t[0:66, :], V_all.t[:, kt, h, :], ptb.t[:, 0:SEG], kt == 0, kt == nkt - 1,
                               [V_all.b, ptb.b], po.b, kt == nkt - 1)
                        def epilogue(po=po, h=h, qsl=qsl, otb=ot[it % 2]):
                            op("act", lambda: nc.scalar.activation(out=rl.t[64:65, :], in_=po.t[64:65, :], func=AF.Ln),
                               reads=[po.b], writes=[rl.b])
                            op("act", lambda: nc.scalar.activation(out=rl.t[64:65, :], in_=rl.t[64:65, :], func=AF.Exp,
                                                                   scale=-1.0), reads=[rl.b], writes=[rl.b])
                            mm(pbc.t[0:64, :], onesf.t[64:65, 0:64], rl.t[64:65, :], True, True, [onesf.b, rl.b], pbc.b, True)
                            op("dve", lambda: nc.vector.tensor_copy(out=osb.t[:], in_=pbc.t[0:64, :]),
                               reads=[pbc.b], writes=[osb.b])
                            op("dve", lambda: nc.vector.tensor_tensor(
                                out=otb.t[:], in0=po.t[0:64, :], in1=osb.t[:], op=ALU.mult),
                               reads=[osb.b, po.b], writes=[otb.b])
                            em.dma("pool", OT[h, :, qsl], otb.t[:], reads=[otb.b], writes=[B_OT])

                        pend_epi.append(epilogue)
                        it += 1
                while pend_epi:
                    pend_epi.pop(0)()
                em.barrier()

        with ExitStack() as stC:
            wstream(seqC, wslot + [SB(stC, f"wslotC{i}", [128, 4096], BF16) for i in range(5)])
            wn2 = SB(stC, "wn2", [128, D], F32)
            em.dma("sp", wn2.t[:], n2w.partition_broadcast(128), writes=[wn2.b])
            x1s = [SB(stC, f"x1_{i}", [128, 4, D], F32) for i in range(2)]
            otls = [SB(stC, f"otl{i}", [64, 8, SEG], BF16) for i in range(2)]
            mxas = [SB(stC, f"mxa{i}", [128, 8, SEG], BF16) for i in range(2)]
            sgbls = [SB(stC, f"sgbl{i}", [128, 8, SEG], BF16) for i in range(2)]
            mixT = SB(stC, "mixT", [128, 8, SEG], BF16)
            tmpc = SB(stC, "tmpc", [128, SEG], F32)
            h2T = SB(stC, "h2T", [128, 8, SEG], BF16)
            actT = SB(stC, "actT", [128, NKC_F, SEG], BF16)
            sil = [SB(stC, f"sil{i}", [128, SEG], BF16) for i in range(2)]
            sqj = SB(stC, "sqjc", [128, D], BF16)
            xs = SB(stC, "xsc", [128, D], BF16)
            ss = SB(stC, "ssc", [128, 1], F32)
            yo = [SB(stC, f"yo{i}", [128, 512], F32) for i in range(2)]
            yic = [0]

            def loadC(j):
                qsl = slice(j * SEG, (j + 1) * SEG)
                x1 = x1s[j % 2]; otl = otls[j % 2]; mxa = mxas[j % 2]; sgbl = sgbls[j % 2]
                em.dma("sp", x1.t[:], xv[S_HALF + j * SEG:S_HALF + (j + 1) * SEG, :].rearrange("(t p) d -> p t d", p=128),
                       writes=[x1.b])
                em.dma("pool", otl.t[:], OT[:, :, qsl].rearrange("h p c -> p h c"), reads=[B_OT], writes=[otl.b])
                em.dma("pool", mxa.t[:], MIXA[:, :, qsl], reads=[B_MIXA], writes=[mxa.b])
                em.dma("pool", sgbl.t[:], SGB[:, :, qsl], reads=[B_SGB], writes=[sgbl.b])

            loadC(0)

            def front(j):
                x1 = x1s[j % 2]; otl = otls[j % 2]; mxa = mxas[j % 2]; sgbl = sgbls[j % 2]
                for half in range(2):
                    sm = wget(17 + half)
                    w = vC(sm)
                    for u in range(4):
                        uu = half * 4 + u
                        p = pf()
                        for h in range(8):
                            mm(p.t[:], w[:, h, u * 128:(u + 1) * 128], otl.t[:, h, :], h == 0, h == 7, [sm.b, otl.b], p.b, h == 7)
                        op("dve", lambda p=p, uu=uu: nc.vector.tensor_tensor(out=tmpc.t[:], in0=p.t[:],
                                                                            in1=sgbl.t[:, uu, :], op=ALU.mult),
                           reads=[p.b, sgbl.b], writes=[tmpc.b])
                        op("pool", lambda uu=uu: nc.gpsimd.tensor_tensor(out=mixT.t[:, uu, :], in0=tmpc.t[:],
                                                                         in1=mxa.t[:, uu, :], op=ALU.add),
                           reads=[tmpc.b, mxa.b], writes=[mixT.b])
                so = [wget(19), wget(20)]
                for t in range(4):
                    for hf in range(2):
                        p = pf()
                        w = vB(so[hf])
                        for kc in range(8):
                            mm(p.t[:], mixT.t[:, kc, t * 128:(t + 1) * 128], w[:, kc, :], kc == 0, kc == 7,
                               [so[hf].b, mixT.b], p.b, kc == 7)
                        op("dve", lambda p=p, t=t, hf=hf: nc.vector.tensor_tensor(
                            out=x1.t[:, t, hf * 512:(hf + 1) * 512], in0=x1.t[:, t, hf * 512:(hf + 1) * 512],
                            in1=p.t[:], op=ALU.add), reads=[x1.b, p.b], writes=[x1.b])
                for t in range(4):
                    norm_T(x1.t[:, t, :], x1.b, wn2, ss, sqj, xs, h2T, t)

            def gateup(j):
                for i in range(6):
                    sgt_ = wget(G_GATE + i)
                    sup = wget(G_UP + i)
                    for u in range(min(4, NKC_F - 4 * i)):
                        uu = 4 * i + u
                        pg_ = proj_fm(sgt_, u, h2T)
                        sl_ = sil[uu % 2]
                        op("act", lambda pg_=pg_, sl_=sl_: nc.scalar.activation(out=sl_.t[:], in_=pg_.t[:], func=AF.Silu),
                           reads=[pg_.b], writes=[sl_.b])
                        pu_ = proj_fm(sup, u, h2T)
                        op("dve", lambda pu_=pu_, sl_=sl_, uu=uu: nc.vector.tensor_tensor(
                            out=actT.t[:, uu, :], in0=pu_.t[:], in1=sl_.t[:], op=ALU.mult),
                           reads=[pu_.b, sl_.b], writes=[actT.b])

            def down(j):
                x1 = x1s[j % 2]; otl = otls[j % 2]; mxa = mxas[j % 2]; sgbl = sgbls[j % 2]
                for hf in range(2):
                    for k in range(3):
                        sl_ = wget(G_DOWN + hf * 3 + k)
                        nk = 8 if k < 2 else NKC_F - 16
                        for t in range(4):
                            p = psF[t]
                            for kk in range(nk):
                                kc = k * 8 + kk
                                mm(p.t[:], actT.t[:, kc, t * 128:(t + 1) * 128], vB(sl_)[:, kk, :], kc == 0,
                                   kc == NKC_F - 1, [sl_.b, actT.b], p.b, kk == nk - 1)
                    for t in range(4):
                        p = psF[t]
                        y = yo[yic[0] % 2]; yic[0] += 1
                        op("dve", lambda p=p, t=t, hf=hf, y=y: nc.vector.tensor_tensor(
                            out=y.t[:], in0=x1.t[:, t, hf * 512:(hf + 1) * 512], in1=p.t[:], op=ALU.add),
                           reads=[x1.b, p.b], writes=[y.b])
                        em.dma("pool", out_d[j * SEG + t * 128:j * SEG + (t + 1) * 128, hf * 512:(hf + 1) * 512],
                               y.t[:], reads=[y.b], writes=[B_out])

            front(0)
            for j in range(8):
                if j + 1 < 8:
                    loadC(j + 1)
                gateup(j)
                if j + 1 < 8:
                    front(j + 1)
                down(j)
            em.barrier()
    return nc


def _const_tables(parity):
    bf = ml_dtypes.bfloat16
    f8 = np.float64
    half = 64
    inv = f8(10000.0) ** (-np.arange(half, dtype=f8) / f8(half))
    pos = np.arange(SV, dtype=f8)
    if parity == 0:
        pos = np.where(pos >= S_HALF, pos - S_HALF, 0.0)
    ang = pos[:, None] * inv[None, :]
    cos = np.cos(ang).T
    sin = np.sin(ang).T
    cosT = np.concatenate([cos, cos], 0)
    sinT = np.concatenate([-sin, sin], 0)
    sc = f8(128.0) ** -0.5
    f32c = lambda a_: np.ascontiguousarray(a_).astype(np.float32)
    t = {"cosk": f32c(cosT * sc), "sink": f32c(sinT * sc),
         "cosq": f32c(cosT[:, S_HALF:]), "sinq": f32c(sinT[:, S_HALF:])}
    q = np.arange(S_HALF)
    vb = 16 + q // 256
    n = np.arange(32)
    valid = n[None, :] < vb[:, None]
    if parity == 0:
        valid &= n[None, :] >= 16
    t["gbias"] = np.where(valid, 0.0, -1e30).astype(np.float32)
    t["ownsel"] = (n[None, :] == vb[:, None]).astype(np.float32)
    lg = np.log1p(-np.exp2(-5.0 - np.arange(4, dtype=f8)))
    idx = np.arange(128, dtype=f8)
    diff = idx[None, :] - idx[:, None]
    dmT = np.where(diff >= 0, np.exp(np.maximum(diff, 0)[None] * lg[:, None, None]), 0.0)
    t["dmatT"] = f32c(dmT.transpose(1, 0, 2).reshape(128, 512))
    xi = np.exp((idx + 1.0)[None, :] * lg[:, None])
    t["xitab"] = f32c(np.broadcast_to(xi.reshape(1, 512), (128, 512)))
    zeta = np.exp((127.0 - idx)[None, :] * lg[:, None])
    t["zetatab"] = f32c(np.repeat(zeta.T[:, :, None], 128, 2).reshape(128, 512))
    t["ident"] = np.eye(128, dtype=np.float32).astype(bf)
    m = np.arange(128)[:, None]; nn = np.arange(256)[None, :]
    t["cmask"] = np.concatenate([(m <= nn), (m + 128 <= nn)], 1).astype(np.float32).astype(bf)
    t["eonehot"] = (np.arange(SV)[None, :] // 256 == np.arange(32)[:, None]).astype(np.float32).astype(bf)
    return t, xi


def make_in_maps(inputs):
    x = np.asarray(inputs["x"], np.float32)
    shared = {k: np.ascontiguousarray(np.asarray(inputs[k], np.float32)[0]) for k in
              ("w_in", "w_ret_out", "w_moba_out", "w_o", "w_ffn_gate", "w_ffn_up", "w_ffn_down")}
    shared["norm1_w"] = np.asarray(inputs["norm1_w"], np.float32).reshape(1, D)
    shared["norm2_w"] = np.asarray(inputs["norm2_w"], np.float32).reshape(1, D)
    shared["q_norm_w"] = np.asarray(inputs["q_norm_w"], np.float32).reshape(64, 1)
    shared["k_norm_w"] = np.asarray(inputs["k_norm_w"], np.float32).reshape(64, 1)
    tabs = []
    for parity in range(2):
        t, xi = _const_tables(parity)
        tabs.append((t, xi))
    maps = []
    for c in range(8):
        b, parity = c // 2, c % 2
        t, xi = tabs[parity]
        m = dict(shared)
        m.update({k: v for k, v in t.items() if v is not None})
        xvv = np.zeros((SV, D), np.float32)
        if parity == 1:
            xvv[:] = x[b]
        else:
            xvv[S_HALF:] = x[b, :S_HALF]
        m["xv"] = xvv
        maps.append(m)
    return maps


_NC_CACHE = {}


def kernel(**inputs):
    if "nc" not in _NC_CACHE:
        _NC_CACHE["nc"] = build()
    nc = _NC_CACHE["nc"]
    maps = make_in_maps(inputs)
    res = run_bass_kernel_spmd(nc, maps, core_ids=list(range(8)))
    B = 4
    out = np.empty((B, 2 * S_HALF, D), np.float32)
    for c in range(8):
        out[c // 2, (c % 2) * S_HALF:(c % 2 + 1) * S_HALF] = np.asarray(res.results[c]["out"], np.float32)
    return out
```
